# Optimizing a Trainium2 kernel written in Bass

```python
import math
import jax, jax.numpy as jnp
from jax import lax
import numpy as np

D_MODEL = 1024
BATCH = 4
SEQ = 8192
DEPTH = 4

N_MIXERS = 4
HEAD_DIM = 64
GROUP_W = D_MODEL // N_MIXERS
HEADS_PER_MIXER = GROUP_W // HEAD_DIM
D_MIX = N_MIXERS * GROUP_W
N_CHUNKS = 13
D_IN = N_CHUNKS * GROUP_W
SHORT_CONV_W = 3
RG_CONV_W = 4
RG_C = 8.0
GMLP_CHUNK = 128
ATTN_PATTERNS = ((128, 1), (512, 4), (2048, 16))
ATTN_BLOCK = 128
NORM_EPS = 1e-6

kernel_name = "hybrid_parallel_groups_conv_rglru_gmlp_dilated_attn"


def rms_norm(x, g):
    xf = x.astype(jnp.float32)
    y = xf * lax.rsqrt(jnp.mean(xf * xf, axis=-1, keepdims=True) + NORM_EPS)
    return (y * g.astype(jnp.float32)).astype(x.dtype)


def causal_depthwise_conv(x, w):
    K, C = w.shape
    return lax.conv_general_dilated(
        x, w[:, None, :].astype(x.dtype), window_strides=(1,), padding=[(K - 1, 0)],
        dimension_numbers=("NWC", "WIO", "NWC"), feature_group_count=C)


def rg_lru(xb, wa, ba, wx, bx, lam):
    B, S, C = xb.shape
    xh = xb.reshape(B, S, HEADS_PER_MIXER, HEAD_DIM)
    r = jax.nn.sigmoid(jnp.einsum('bshi,hij->bshj', xh, wa).reshape(B, S, C) + ba)
    i = jax.nn.sigmoid(jnp.einsum('bshi,hij->bshj', xh, wx).reshape(B, S, C) + bx)
    log_a = (-RG_C * r.astype(jnp.float32)) * jax.nn.softplus(-lam.astype(jnp.float32))
    a = jnp.exp(log_a)
    mult = jnp.sqrt(-jnp.expm1(2.0 * log_a))
    b = mult * (i * xb).astype(jnp.float32)

    def combine(e1, e2):
        a1, b1 = e1
        a2, b2 = e2
        return a1 * a2, a2 * b1 + b2

    _, h = lax.associative_scan(combine, (a, b), axis=1)
    return h.astype(xb.dtype)


def dilated_window_attention(q, k, v, slopes, window, dil):
    B, S, H, Dh = q.shape
    n_back = window // dil
    span = dil * ATTN_BLOCK
    Sp = -(-S // span) * span
    pad = Sp - S
    if pad:
        pw = ((0, 0), (0, pad), (0, 0), (0, 0))
        q, k, v = jnp.pad(q, pw), jnp.pad(k, pw), jnp.pad(v, pw)
    L = Sp // dil
    nb = L // ATTN_BLOCK
    to_blocks = lambda t: t.reshape(B, nb, ATTN_BLOCK, dil, H, Dh)
    qb, kb, vb = to_blocks(q), to_blocks(k), to_blocks(v)

    def with_prev(t):
        prev = jnp.pad(t[:, :-1], ((0, 0), (1, 0), (0, 0), (0, 0), (0, 0), (0, 0)))
        return jnp.concatenate([prev, t], axis=2)

    kk, vv = with_prev(kb), with_prev(vb)
    scale = 1.0 / math.sqrt(Dh)
    s = jnp.einsum('bnqrhd,bnkrhd->bnrhqk', qb.astype(jnp.float32),
                   kk.astype(jnp.float32)) * scale
    qi = jnp.arange(ATTN_BLOCK)[:, None]
    ki = jnp.arange(2 * ATTN_BLOCK)[None, :]
    delta = qi + ATTN_BLOCK - ki
    band = (delta >= 0) & (delta <= n_back)
    first_ok = (jnp.arange(nb)[:, None, None] > 0) | (ki[None] >= ATTN_BLOCK)
    mask = band[None] & first_ok
    bias = -slopes[:, None, None] * (delta * dil).astype(jnp.float32)[None]
    s = jnp.where(mask[None, :, None, None], s + bias, -jnp.inf)
    m = jnp.max(s, axis=-1, keepdims=True)
    p = jnp.exp(s - m)
    l = jnp.sum(p, axis=-1, keepdims=True)
    o = jnp.einsum('bnrhqk,bnkrhd->bnqrhd', p, vv.astype(jnp.float32))
    o = o / jnp.transpose(l, (0, 1, 4, 2, 3, 5))
    lse = jnp.transpose((m + jnp.log(l))[..., 0], (0, 1, 4, 2, 3))
    o = o.reshape(B, Sp, H, Dh)[:, :S]
    lse = lse.reshape(B, Sp, H)[:, :S]
    return o, lse


def setup_inputs(seed: int = 0) -> dict:
    key = jax.random.key(seed)
    ks = jax.random.split(key, 16)
    f32 = jnp.float32
    nrm = lambda k, shape, sc: jax.random.normal(k, shape, f32) * sc
    x = jax.random.normal(ks[0], (BATCH, SEQ, D_MODEL), f32)
    norm_g = 1.0 + nrm(ks[1], (DEPTH, D_MODEL), 0.01)
    w_in = nrm(ks[2], (DEPTH, D_MODEL, D_IN), D_MODEL ** -0.5)
    conv_a_w = nrm(ks[3], (DEPTH, SHORT_CONV_W, GROUP_W), SHORT_CONV_W ** -0.5)
    conv_r_w = nrm(ks[4], (DEPTH, RG_CONV_W, GROUP_W), RG_CONV_W ** -0.5)
    conv_r_b = nrm(ks[5], (DEPTH, GROUP_W), 0.01)
    lru_wa = nrm(ks[6], (DEPTH, HEADS_PER_MIXER, HEAD_DIM, HEAD_DIM), HEAD_DIM ** -0.5)
    lru_ba = nrm(ks[7], (DEPTH, GROUP_W), 0.01)
    lru_wx = nrm(ks[8], (DEPTH, HEADS_PER_MIXER, HEAD_DIM, HEAD_DIM), HEAD_DIM ** -0.5)
    lru_bx = nrm(ks[9], (DEPTH, GROUP_W), 0.01)
    a_c = jax.random.uniform(ks[10], (DEPTH, GROUP_W), f32, 0.9, 0.999)
    sig = a_c ** (1.0 / RG_C)
    lru_lambda = jnp.log(sig) - jnp.log1p(-sig)
    gmlp_norm_g = 1.0 + nrm(ks[11], (DEPTH, GROUP_W), 0.01)
    gmlp_ws = nrm(ks[12], (DEPTH, HEADS_PER_MIXER, GMLP_CHUNK, GMLP_CHUNK), GMLP_CHUNK ** -0.5)
    gmlp_bs = 1.0 + nrm(ks[13], (DEPTH, HEADS_PER_MIXER, GMLP_CHUNK), 0.1)
    w_out = nrm(ks[14], (DEPTH, D_MIX, D_MODEL), D_MIX ** -0.5)
    final_g = 1.0 + nrm(ks[15], (D_MODEL,), 0.01)
    return {"x": x, "norm_g": norm_g, "w_in": w_in, "conv_a_w": conv_a_w,
            "conv_r_w": conv_r_w, "conv_r_b": conv_r_b, "lru_wa": lru_wa,
            "lru_ba": lru_ba, "lru_wx": lru_wx, "lru_bx": lru_bx,
            "lru_lambda": lru_lambda, "gmlp_norm_g": gmlp_norm_g,
            "gmlp_ws": gmlp_ws, "gmlp_bs": gmlp_bs, "w_out": w_out,
            "final_g": final_g}


def reference(x, norm_g, w_in, conv_a_w, conv_r_w, conv_r_b, lru_wa, lru_ba,
              lru_wx, lru_bx, lru_lambda, gmlp_norm_g, gmlp_ws, gmlp_bs, w_out,
              final_g):
    B, S, _ = x.shape
    H = HEADS_PER_MIXER
    slopes = 2.0 ** (-8.0 * jnp.arange(1, H + 1, dtype=jnp.float32) / H)
    causal_chunk = jnp.tril(jnp.ones((GMLP_CHUNK, GMLP_CHUNK), dtype=bool))
    n_chunk = S // GMLP_CHUNK
    for l in range(DEPTH):
        h = rms_norm(x, norm_g[l])
        z = jnp.einsum('bsd,de->bse', h, w_in[l])
        (a_x, a_b, a_c, a_g,
         r_x, r_g,
         c_u, c_v, c_g,
         d_q, d_k, d_v, d_g) = jnp.split(z, N_CHUNKS, axis=-1)

        y_a = a_b * causal_depthwise_conv(a_c * a_x, conv_a_w[l]) * jax.nn.silu(a_g)

        xb = causal_depthwise_conv(r_x, conv_r_w[l]) + conv_r_b[l]
        y_b = rg_lru(xb, lru_wa[l], lru_ba[l], lru_wx[l], lru_bx[l], lru_lambda[l]) * jax.nn.silu(r_g)

        u = jax.nn.gelu(c_u)
        vv = rms_norm(jax.nn.gelu(c_v), gmlp_norm_g[l])
        vv = vv.reshape(B, n_chunk, GMLP_CHUNK, H, HEAD_DIM)
        ws = jnp.where(causal_chunk[None], gmlp_ws[l], 0.0).astype(vv.dtype)
        sp = jnp.einsum('hts,bnshc->bnthc', ws, vv) + jnp.transpose(gmlp_bs[l])[:, :, None]
        y_c = u * sp.reshape(B, S, GROUP_W) * jax.nn.silu(c_g)

        q = d_q.reshape(B, S, H, HEAD_DIM)
        k = d_k.reshape(B, S, H, HEAD_DIM)
        v = d_v.reshape(B, S, H, HEAD_DIM)
        outs, lses = [], []
        for window, dil in ATTN_PATTERNS:
            o_p, lse_p = dilated_window_attention(q, k, v, slopes, window, dil)
            outs.append(o_p)
            lses.append(lse_p)
        wts = jax.nn.softmax(jnp.stack(lses), axis=0)
        o = jnp.einsum('pbsh,pbshd->bshd', wts, jnp.stack(outs))
        y_d = o.reshape(B, S, GROUP_W).astype(x.dtype) * jax.nn.silu(d_g)

        y = jnp.concatenate([y_a, y_b, y_c, y_d], axis=-1)
        x = x + jnp.einsum('bse,ed->bsd', y, w_out[l])
    return rms_norm(x, final_g)
```

```python
import numpy as np
from contextlib import ExitStack
import concourse.bass as bass
import concourse.mybir as mybir
from concourse.bass_utils import run_bass_kernel_spmd

F32 = mybir.dt.float32
BF16 = mybir.dt.bfloat16
AF = mybir.ActivationFunctionType
ALU = mybir.AluOpType

D = 1024
DIN = 3328
G = 512
NPP = 22
EPS = 1e-6
REORDER = True


class Buf:
    __slots__ = ("name", "w", "r")

    def __init__(self, name):
        self.name = name
        self.w = None
        self.r = []


class Op:
    __slots__ = ("eng", "fn", "deps", "is_dma", "needs_inc", "inc_idx",
                 "dma_sem", "dma_val", "clk", "cost", "lat", "idx", "t0", "t1", "succ", "npend", "prevdma")

    def __init__(self, eng, fn, is_dma):
        self.eng = eng
        self.fn = fn
        self.is_dma = is_dma
        self.deps = []
        self.needs_inc = False
        self.inc_idx = 0
        self.dma_sem = -1
        self.dma_val = 0
        self.clk = None
        self.succ = []
        self.prevdma = None


class Prog:
    ENGS = ("pe", "act", "dve", "pool", "sp")
    NDMA = {"sp": 16, "pool": 8, "act": 4}

    def __init__(self, nc):
        self.nc = nc
        self.all = []
        self.out_dmas = []
        self.last_dma = {}

    def add(self, eng, fn, reads=(), writes=(), dma=False, is_out=False, cost=200.0, lat=0.0):
        op = Op(eng, fn, dma)
        op.cost = cost
        op.lat = lat
        op.idx = len(self.all)
        deps = {}
        for b in reads:
            if b.w is not None:
                deps[id(b.w)] = (b.w, True)
        for b in writes:
            if b.w is not None and id(b.w) not in deps:
                deps[id(b.w)] = (b.w, False)
            for r in b.r:
                if id(r) not in deps:
                    deps[id(r)] = (r, False)
        for d, raw in deps.values():
            if d is op:
                continue
            op.deps.append((d, raw))
        for b in reads:
            b.r.append(op)
        for b in writes:
            b.w = op
            b.r = []
        self.all.append(op)
        if is_out:
            self.out_dmas.append(op)
        return op

    def schedule(self):
        import heapq
        ops = self.all
        for op in ops:
            op.npend = len(op.deps) + (1 if op.prevdma is not None else 0)
            for d, _ in op.deps:
                d.succ.append(op)
        nxt = {}
        for op in ops:
            if op.prevdma is not None:
                nxt[id(op.prevdma)] = op
        bl = {}
        for op in reversed(ops):
            m = 0.0
            for s2 in op.succ:
                v = bl[id(s2)]
                if v > m:
                    m = v
            bl[id(op)] = op.cost + op.lat + m
        for op in ops:
            op.idx = (-bl[id(op)], op.idx)
        ready = {e: [] for e in self.ENGS}
        rtime = {}
        for op in ops:
            if op.npend == 0:
                heapq.heappush(ready[op.eng], (op.idx, op))
                rtime[id(op)] = 0.0
        free = {e: 0.0 for e in self.ENGS}
        order = []
        nleft = len(ops)
        while nleft:
            best = None
            for e in self.ENGS:
                h = ready[e]
                if not h:
                    continue
                cand = heapq.nsmallest(6, h)
                pick = None
                for idx, op in cand:
                    if rtime[id(op)] <= free[e]:
                        pick = op
                        break
                if pick is None:
                    pick = min((c[1] for c in cand), key=lambda o: (rtime[id(o)], o.idx))
                st = max(free[e], rtime[id(pick)])
                if best is None or st < best[0] or (st == best[0] and pick.idx < best[2].idx):
                    best = (st, e, pick)
            st, e, op = best
            h = ready[e]
            h.remove((op.idx, op))
            heapq.heapify(h)
            op.t0 = st
            free[e] = st + op.cost
            op.t1 = st + op.cost + op.lat
            order.append(op)
            nleft -= 1
            rel = list(op.succ)
            for s2 in rel:
                s2.npend -= 1
                rtime[id(s2)] = max(rtime.get(id(s2), 0.0), op.t1)
                if s2.npend == 0:
                    heapq.heappush(ready[s2.eng], (s2.idx, s2))
            n2 = nxt.get(id(op))
            if n2 is not None:
                n2.npend -= 1
                rtime[id(n2)] = max(rtime.get(id(n2), 0.0), op.t0)
                if n2.npend == 0:
                    heapq.heappush(ready[n2.eng], (n2.idx, n2))
        self.makespan = max(o.t1 for o in order)
        return order

    def emit(self, reorder=True):
        nc = self.nc
        if reorder:
            order = self.schedule()
        else:
            order = list(self.all)
        self.ops = {e: [] for e in self.ENGS}
        for op in order:
            self.ops[op.eng].append(op)
        dma_rr = {e: 0 for e in self.ENGS}
        dma_cnt = {}
        dma_prev = {}
        sem_deps = {}
        for op in order:
            sd = []
            for d, raw in op.deps:
                if d.is_dma or d.eng != op.eng or op.eng != "pe":
                    sd.append(d)
                    d.needs_inc = True
            if op.is_dma:
                n = self.NDMA[op.eng]
                s = dma_rr[op.eng] % n
                dma_rr[op.eng] += 1
                key = (op.eng, s)
                prev = dma_prev.get(key)
                if prev is not None:
                    sd.append(prev)
                dma_cnt[key] = dma_cnt.get(key, 0) + 1
                op.dma_sem = key
                op.dma_val = 16 * dma_cnt[key]
                dma_prev[key] = op
            sem_deps[id(op)] = sd
        cnt = {e: 0 for e in self.ENGS}
        for e in self.ENGS:
            for op in self.ops[e]:
                if op.is_dma:
                    continue
                if op.needs_inc:
                    cnt[e] += 1
                op.inc_idx = cnt[e]
        known = {e: {} for e in self.ENGS}
        waits = {}
        for op in order:
            kn = known[op.eng]
            m = {}
            for d in sem_deps[id(op)]:
                if d.is_dma:
                    key = ("dma",) + d.dma_sem
                    val = d.dma_val
                else:
                    key = d.eng
                    val = d.inc_idx
                if kn.get(key, 0) >= val:
                    continue
                m[key] = max(m.get(key, 0), val)
                for k, v in d.clk.items():
                    if kn.get(k, 0) < v:
                        kn[k] = v
                if kn.get(key, 0) < val:
                    kn[key] = val
            waits[id(op)] = list(m.items())
            c = dict(kn)
            if not op.is_dma:
                c[op.eng] = max(c.get(op.eng, 0), op.inc_idx)
            op.clk = c

        with ExitStack() as st:
            sems = {e: st.enter_context(nc.semaphore("s_" + e)) for e in self.ENGS}
            dsems = {}
            for e, n in self.NDMA.items():
                for i in range(n):
                    dsems[(e, i)] = st.enter_context(nc.semaphore("d_%s%d" % (e, i)))
            block = st.enter_context(nc.Block())

            def body(ename):
                def f(eng):
                    for op in self.ops[ename]:
                        for key, val in waits[id(op)]:
                            if isinstance(key, tuple):
                                eng.wait_ge(dsems[key[1:]], val)
                            else:
                                eng.wait_ge(sems[key], val)
                        ins = op.fn(eng)
                        if op.is_dma:
                            ins.then_inc(dsems[op.dma_sem], 16)
                        elif op.needs_inc:
                            ins.then_inc(sems[ename], 1)
                    if ename == "sp":
                        last = {}
                        for op in self.out_dmas:
                            last[op.dma_sem] = max(last.get(op.dma_sem, 0), op.dma_val)
                        for s, v in last.items():
                            eng.wait_ge(dsems[s], v)
                return f

            block.tensor(body("pe"))
            block.scalar(body("act"))
            block.vector(body("dve"))
            block.gpsimd(body("pool"))
            block.sync(body("sp"))


def build_nc(L, S):
    NG = S // G
    nc = bass.Bass("TRN2", target_bir_lowering=False)
    dt = lambda n, s, d=F32, k="ExternalInput": nc.dram_tensor(n, s, d, kind=k).ap()
    x_d = dt("x", [S, D])
    win_d = dt("w_in", [L, D, DIN])
    wout_d = dt("w_out", [L, D, D])
    ngb_d = dt("ngb", [L, 128, D])
    fgb_d = dt("fgb", [128, D])
    pp_d = dt("pp", [128, L, NPP])
    wbd_d = dt("wbd", [L, 128, 4, 128])
    ggb_d = dt("ggb", [L, 128, 256])
    wst_d = dt("wst", [L, 128, 4, 128])
    bsb_d = dt("bsb", [L, 128, 2, 512])
    tab_d = dt("tab", [128, 24, 128])
    cst_d = dt("cst", [128, 2, 128])
    out_d = dt("out", [S, D], F32, "ExternalOutput")
    xs_d = dt("xs", [S, D], F32, "Internal")

    with ExitStack() as st:
        def sb(n, s, d=F32):
            return st.enter_context(nc.sbuf_tensor(n, s, d))

        def ps(n, s, d=F32):
            return st.enter_context(nc.psum_tensor(n, s, d))

        P = Prog(nc)

        w_in = sb("w_in_sb", [128, 8, DIN], BF16)
        w_out = sb("w_out_sb", [128, 8, D], BF16)
        xa = [sb("xa%d" % i, [128, D]) for i in range(2)]
        xb2 = [sb("xb%d" % i, [128, D]) for i in range(2)]
        ht = [sb("ht%d" % i, [128, D], BF16) for i in range(2)]
        hTs = [sb("hT%d" % i, [128, 8, G], BF16) for i in range(1)]
        ngb = sb("ngb_sb", [128, D])
        fgb = sb("fgb_sb", [128, D])
        pp = sb("pp_sb", [128, L, NPP])
        lc = sb("lc", [128, 2])
        lct = sb("lct", [128, 2])
        wbd = sb("wbd_sb", [128, 4, 128], BF16)
        ggb = sb("ggb_sb", [128, 256])
        wstf = sb("wstf", [128, 4, 128], BF16)
        wst = sb("wst_sb", [128, 4, 128], BF16)
        bsb = sb("bsb_sb", [128, 2, 128])
        tab = sb("tab_sb", [128, 24, 128], BF16)
        cstf = sb("cstf", [128, 2, 128])
        ident = sb("ident", [128, 128], BF16)
        ss = sb("ss", [128, 16])
        rt = sb("rt", [128, 16])
        rs = sb("rs", [128, 16])
        ssv = sb("ssv", [128, 4])
        rtv = sb("rtv", [128, 4])
        rsv = sb("rsv", [128, 4])
        NSC = 10
        SC = [sb("sc%d" % i, [128, G]) for i in range(NSC)]
        tA = [sb("tA%d" % i, [128, 2 + G]) for i in range(2)]
        xr = [sb("xr%d" % i, [128, 3 + G]) for i in range(2)]
        hst = sb("hst", [128, 2])
        xbbs = [sb("xbb%d" % i, [128, G], BF16) for i in range(2)]
        cpow = sb("cpow", [128, 2])
        lch = sb("lch", [128, 2])
        bh = sb("bh", [128, 4])
        vvt = [sb("vvt%d" % i, [128, 256], BF16) for i in range(4)]
        YTs = [[sb("yT%d_%d" % (p, i), [128, G], BF16) for i in range(8)] for p in range(2)]
        Qbd = [sb("Qbd%d" % i, [128, 2, G], BF16) for i in range(2)]
        sgd = [sb("sgd%d" % i, [128, G]) for i in range(2)]
        kwin = [sb("kwin%d" % i, [128, 2560], BF16) for i in range(2)]
        vwin = [sb("vwin%d" % i, [128, 2560], BF16) for i in range(2)]
        vaug = sb("vaug", [128, 16, 2, 128], BF16)
        NPT = 6
        ptl = [sb("pt%d" % i, [128, G], BF16) for i in range(NPT)]
        PS = [ps("ps%d" % i, [128, G]) for i in range(7)]
        PT = ps("pt", [128, 1024], BF16)

        bf = Buf
        B_win = [bf("w_in%d" % c) for c in range(13)]
        B_wout = bf("w_out")
        B_xa = [bf("xa0"), bf("xa1")]
        B_xb = [bf("xb0"), bf("xb1")]
        B_ht = [bf("ht0"), bf("ht1")]
        B_hTs = [[bf("hT%d_%d" % (p, i)) for i in range(4)] for p in range(1)]
        B_ngb = bf("ngb"); B_fgb = bf("fgb"); B_pp = bf("pp"); B_lc = bf("lc"); B_lct = bf("lct")
        B_wbd = bf("wbd"); B_ggb = bf("ggb"); B_wstf = bf("wstf"); B_wst = bf("wst"); B_bsb = bf("bsb")
        B_tab = bf("tab"); B_cstf = bf("cstf"); B_ident = bf("ident")
        junk = ht[1]; B_junk = B_ht[1]
        B_ss = [bf("ss%d" % i) for i in range(16)]; B_rt = [bf("rt%d" % i) for i in range(16)]; B_rs = [bf("rs%d" % i) for i in range(16)]
        B_ssv = bf("ssv"); B_rtv = bf("rtv"); B_rsv = bf("rsv")
        B_SC = [bf("sc%d" % i) for i in range(NSC)]
        B_tA = [bf("tA0"), bf("tA1")]
        B_xr = [bf("xr0"), bf("xr1")]
        B_hst = bf("hst"); B_xbbs = [bf("xbb0"), bf("xbb1")]; B_cpow = bf("cpow"); B_lch = bf("lch"); B_bh = bf("bh")
        B_vvt = [bf("vvt%d" % i) for i in range(4)]
        B_YTs = [[bf("yT%d_%d" % (p, i)) for i in range(8)] for p in range(2)]
        B_Qbd = [bf("Qbd0"), bf("Qbd1")]
        B_sgd = [bf("sgd0"), bf("sgd1")]
        B_kwo = [bf("kwo0"), bf("kwo1")]; B_kwn = [bf("kwn0"), bf("kwn1")]
        B_vwo = [bf("vwo0"), bf("vwo1")]; B_vwn = [bf("vwn0"), bf("vwn1")]
        B_VB = [bf("vb%d" % i) for i in range(16)]
        B_ptl = [bf("pt%d" % i) for i in range(6)]
        B_PS = [bf("ps%d" % i) for i in range(7)]
        B_PT = bf("pt")
        B_XD = [bf("xd%d" % t) for t in range(NG * 4)]

        def fs(ap):
            n = 1
            for d in ap.shape[1:]:
                n *= d
            return n

        def dma(out, in_, R, W, eng="sp", is_out=False):
            nbytes = fs(out) * out.shape[0] * 4
            P.add(eng, lambda e: e.dma_start(out=out, in_=in_), R, W, dma=True, is_out=is_out,
                  cost=(150.0 if eng == "sp" else 1500.0), lat=2500.0 + nbytes / 100.0)

        def act(out, in_, func, R, W, **kw):
            P.add("act", lambda e: e.activation(out=out, in_=in_, func=func, **kw), R, W,
                  cost=220.0 + fs(out) / 1.4, lat=270.0)

        def vcost(eng, out):
            if eng == "dve":
                return 120.0 + fs(out) / 0.96
            return 350.0 + 1.9 * fs(out)

        def tt(eng, out, in0, in1, op, R, W):
            P.add(eng, lambda e: e.tensor_tensor(out=out, in0=in0, in1=in1, op=op), R, W, cost=vcost(eng, out), lat=270.0)

        def ts(eng, out, in0, s1, s2, op0, op1, R, W):
            if s2 is None:
                P.add(eng, lambda e: e.tensor_scalar(out=out, in0=in0, scalar1=s1, scalar2=None, op0=op0), R, W,
                      cost=vcost(eng, out), lat=270.0)
            else:
                P.add(eng, lambda e: e.tensor_scalar(out=out, in0=in0, scalar1=s1, scalar2=s2, op0=op0, op1=op1),
                      R, W, cost=vcost(eng, out), lat=270.0)

        def stt(eng, out, in0, scalar, in1, op0, op1, R, W):
            P.add(eng, lambda e: e.scalar_tensor_tensor(out=out, in0=in0, scalar=scalar, in1=in1, op0=op0, op1=op1),
                  R, W, cost=vcost(eng, out), lat=270.0)

        def mm(out, lhsT, rhs, start, stop, R, W, skip=False):
            P.add("pe", lambda e: e.matmul(out, lhsT=lhsT, rhs=rhs, start=start, stop=stop,
                                           skip_group_check=skip), R, W, cost=60.0 + max(64, fs(rhs)) / 2.4, lat=270.0)

        def tr(out, in_, idn, R, W):
            P.add("pe", lambda e: e.transpose(out, in_, idn), R, W, cost=110.0, lat=270.0)

        def cp(eng, out, in_, R, W):
            P.add(eng, lambda e: e.tensor_copy(out=out, in_=in_), R, W, cost=vcost(eng, out), lat=270.0)

        def ms(eng, out, val, W):
            P.add(eng, lambda e: e.memset(out, val), (), W, cost=vcost(eng, out))

        def recip(out, in_, R, W):
            P.add("dve", lambda e: e.reciprocal(out=out, in_=in_), R, W, cost=150.0 + 6.3 * fs(out), lat=270.0)

        bank_rr = [0]

        def nb():
            i = bank_rr[0] % 7
            bank_rr[0] += 1
            return PS[i], B_PS[i]

        dma(cstf[:], cst_d[:, :, :], [], [B_cstf])
        cp("dve", ident[:], cstf[:, 0, :], [B_cstf], [B_ident])
        dma(tab[:], tab_d[:, :, :], [], [B_tab], eng="pool")
        dma(pp[:], pp_d[:, :, :], [], [B_pp])
        dma(fgb[:], fgb_d[:, :], [], [B_fgb])
        ms("pool", vaug[:], 1.0, B_VB)
        for hh in range(2):
            ms("pool", Qbd[hh][:], 0.0, [B_Qbd[hh]])
        ms("pool", cpow[:, 0:1], -0.5, [B_cpow])
        ms("pool", cpow[:, 1:2], 0.5, [B_cpow])

        MUL, ADD = ALU.mult, ALU.add

        def ppow(out, in_, which, R, W, n=1):
            ex = cpow[:, which:which + 1] if n == 1 else cpow[:, which:which + 1].broadcast_to([128, n])
            P.add("pool", lambda e: e.tensor_tensor(out=out, in0=in_, in1=ex, op=ALU.pow), R + [B_cpow], W,
                  cost=250.0 + n / 0.96)

        def rmsnorm_stats(xap, bx, c):
            act(junk[:], xap, AF.Square, [bx], [B_junk, B_ss[c]], accum_out=ss[:, c:c + 1])
            ts("dve", rt[:, c:c + 1], ss[:, c:c + 1], 1.0 / D, EPS, MUL, ADD, [B_ss[c]], [B_rt[c]])
            ppow(rs[:, c:c + 1], rt[:, c:c + 1], 0, [B_rt[c]], [B_rs[c]])

        for l in range(L):
            last_layer = (l == L - 1)
            for c in range(13):
                for kc in range(8):
                    dma(w_in[:, kc, c * 256:(c + 1) * 256],
                        win_d[l, kc * 128:(kc + 1) * 128, c * 256:(c + 1) * 256], [], [B_win[c]], eng="pool")
            for kc in range(8):
                dma(w_out[:, kc, :], wout_d[l, kc * 128:(kc + 1) * 128, :], [], [B_wout], eng="pool")
            dma(ngb[:], ngb_d[l, :, :], [], [B_ngb])
            dma(wbd[:], wbd_d[l, :, :, :], [], [B_wbd], eng="pool")
            dma(ggb[:], ggb_d[l, :, :], [], [B_ggb])
            dma(wstf[:], wst_d[l, :, :, :], [], [B_wstf], eng="pool")
            dma(bsb[:], bsb_d[l, :, :, 0:128], [], [B_bsb])
            for h in range(4):
                tt("dve", wst[:, h, :], wstf[:, h, :], cstf[:, 1, :], MUL, [B_wstf, B_cstf], [B_wst])
            act(lct[:], pp[:, l, 20:22], AF.Exp, [B_pp], [B_lct], scale=-1.0)
            act(lct[:], lct[:], AF.Ln, [B_lct], [B_lct], bias=1.0)
            ts("dve", lc[:], lct[:], -8.0, None, MUL, None, [B_lct], [B_lc])
            ts("dve", lch[:], lct[:], -4.0, None, MUL, None, [B_lct], [B_lch])
            ts("dve", bh[:], pp[:, l, 16:20], 0.5, None, MUL, None, [B_pp], [B_bh])
            ts("pool", ggb[:], ggb[:], 0.5, None, MUL, None, [B_ggb], [B_ggb])
            for hh in range(2):
                ms("pool", tA[hh][:, 0:2], 0.0, [B_tA[hh]])
                ms("pool", xr[hh][:, 0:3], 0.0, [B_xr[hh]])
                ms("pool", kwin[hh][:, 0:2048], 0.0, [B_kwo[hh]])
                ms("pool", vwin[hh][:, 0:2048], 0.0, [B_vwo[hh]])
            ms("pool", hst[:], 0.0, [B_hst])

            def PPc(col):
                return pp[:, l, col:col + 1]

            for g in range(NG):
                src = x_d if l == 0 else xs_d
                hT = hTs[0]
                YT = YTs[g % 2]
                B_YT = B_YTs[g % 2]
                B_hT = B_hTs[0]
                for j in range(4):
                    T = g * 4 + j
                    xb_ = xa[T % 2]
                    bx = B_xa[T % 2]
                    c = T % 8
                    dma(xb_[:], src[T * 128:(T + 1) * 128, :], [B_XD[T]] if l > 0 else [], [bx])
                    rmsnorm_stats(xb_[:], bx, c)
                    hb = ht[j % 2]
                    stt("dve", hb[:], xb_[:], rs[:, c:c + 1], ngb[:], MUL, MUL,
                        [bx, B_rs[c], B_ngb], [B_ht[j % 2]])
                    for kc in range(8):
                        tr(PT[:, kc * 128:(kc + 1) * 128], hb[:, kc * 128:(kc + 1) * 128], ident[:],
                           [B_ht[j % 2], B_ident], [B_PT])
                    P.add("act", (lambda jj, hTt: (lambda e: e.activation(
                        out=hTt[:, :, jj * 128:(jj + 1) * 128],
                        in_=PT[:, :].rearrange("p (k t) -> p k t", k=8), func=AF.Copy)))(j, hT),
                        [B_PT], [B_hT[j]], cost=220.0 + 1024 / 1.4)

                def S_(hh, t):
                    return SC[hh * 5 + t], B_SC[hh * 5 + t]

                def inproj(ch, hh):
                    bk, bb = nb()
                    ct = ch * 2 + hh
                    for kc in range(8):
                        mm(bk[:], w_in[:, kc, ct * 128:(ct + 1) * 128], hT[:, kc, :], kc == 0, kc == 7,
                           [B_win[ch]] + B_hT, [bb])
                    return bk, bb

                def silu2(dst, bdst, pg):
                    act(dst, pg[0][:], AF.Tanh, [pg[1]], [bdst], scale=0.5)
                    stt("dve", dst, dst, 1.0, pg[0][:], ADD, MUL, [bdst, pg[1]], [bdst])

                def gelu2(src, n, t_in, b_in, t_out, b_out):
                    act(t_in, src[0], AF.Square, [src[1]], [b_in])
                    ts("dve", t_in, t_in, 0.044715, 1.0, MUL, ADD, [b_in], [b_in])
                    tt("dve", t_in, t_in, src[0], MUL, [b_in, src[1]], [b_in])
                    act(t_out, t_in, AF.Tanh, [b_in], [b_out], scale=0.7978845608028654)
                    stt("dve", t_out, t_out, 1.0, src[0], ADD, MUL, [b_out, src[1]], [b_out])

                for hh in range(2):
                    px = inproj(0, hh)
                    pb = inproj(1, hh)
                    pc = inproj(2, hh)
                    pg = inproj(3, hh)
                    (s0, b0), (s1, b1) = S_(hh, 0), S_(hh, 1)
                    act(s0[:], px[0][:], AF.Copy, [px[1]], [b0])
                    tt("dve", tA[hh][:, 2:2 + G], pc[0][:], s0[:], MUL, [pc[1], b0], [B_tA[hh]])
                    silu2(s1[:], b1, pg)
                    ts("dve", s0[:], tA[hh][:, 2:2 + G], PPc(hh * 3 + 2), None, MUL, None, [B_tA[hh], B_pp], [b0])
                    stt("dve", s0[:], tA[hh][:, 1:1 + G], PPc(hh * 3 + 1), s0[:], MUL, ADD, [B_tA[hh], B_pp, b0], [b0])
                    stt("dve", s0[:], tA[hh][:, 0:G], PPc(hh * 3 + 0), s0[:], MUL, ADD, [B_tA[hh], B_pp, b0], [b0])
                    cp("pool", tA[hh][:, 0:2], tA[hh][:, G:G + 2], [B_tA[hh]], [B_tA[hh]])
                    tt("dve", s0[:], s0[:], pb[0][:], MUL, [b0, pb[1]], [b0])
                    stt("dve", YT[0 + hh][:], s0[:], 0.5, s1[:], MUL, MUL, [b0, b1], [B_YT[0 + hh]])

                for hh in range(2):
                    px = inproj(4, hh)
                    pg = inproj(5, hh)
                    (sc_, bc_), (sa_, ba_), (sm_, bm_), (si_, bi_), (sh_, bh_) = [S_(hh, t) for t in range(5)]
                    xbb = xbbs[hh]
                    B_xbb = B_xbbs[hh]
                    act(xr[hh][:, 3:3 + G], px[0][:], AF.Copy, [px[1]], [B_xr[hh]])
                    cb = 6 + hh * 4
                    ts("dve", sc_[:], xr[hh][:, 3:3 + G], PPc(cb + 3), PPc(14 + hh), MUL, ADD, [B_xr[hh], B_pp], [bc_])
                    for k in (2, 1, 0):
                        stt("dve", sc_[:], xr[hh][:, k:k + G], PPc(cb + k), sc_[:], MUL, ADD,
                            [B_xr[hh], B_pp, bc_], [bc_])
                    cp("pool", xr[hh][:, 0:3], xr[hh][:, G:G + 3], [B_xr[hh]], [B_xr[hh]])
                    act(xbb[:], sc_[:], AF.Copy, [bc_], [B_xbb])
                    bkr, bbr = nb()
                    mm(bkr[:], wbd[:, 0 + hh, :], xbb[:], True, True, [B_wbd, B_xbb], [bbr])
                    bki, bbi = nb()
                    mm(bki[:], wbd[:, 2 + hh, :], xbb[:], True, True, [B_wbd, B_xbb], [bbi])
                    act(sa_[:], bkr[:], AF.Tanh, [bbr, B_bh], [ba_], scale=0.5, bias=bh[:, hh:hh + 1])
                    act(si_[:], bki[:], AF.Tanh, [bbi, B_bh], [bi_], scale=0.5, bias=bh[:, 2 + hh:3 + hh])
                    act(sm_[:], sa_[:], AF.Exp, [ba_, B_lc], [bm_], scale=lc[:, hh:hh + 1], bias=lc[:, hh:hh + 1])
                    act(sa_[:], sa_[:], AF.Exp, [ba_, B_lch], [ba_], scale=lch[:, hh:hh + 1], bias=lch[:, hh:hh + 1])
                    ts("dve", sm_[:], sm_[:], 1.0, 1.0, ALU.min, ALU.subtract, [bm_], [bm_])
                    act(sm_[:], sm_[:], AF.Sqrt, [bm_], [bm_], scale=-1.0)
                    stt("dve", si_[:], si_[:], 1.0, sc_[:], ADD, MUL, [bi_, bc_], [bi_])
                    stt("dve", sm_[:], sm_[:], 0.5, si_[:], MUL, MUL, [bm_, bi_], [bm_])
                    P.add("dve", (lambda hh_, a_, m_, h_: (lambda e: e.tensor_tensor_scan(
                        out=h_[:], data0=a_[:], data1=m_[:], initial=hst[:, hh_:hh_ + 1],
                        op0=MUL, op1=ADD)))(hh, sa_, sm_, sh_),
                        [ba_, bm_, B_hst], [bh_], cost=120.0 + 2 * G / 0.96)
                    cp("pool", hst[:, hh:hh + 1], sh_[:, G - 1:G], [bh_], [B_hst])
                    silu2(si_[:], bi_, pg)
                    stt("dve", YT[2 + hh][:], sh_[:], 0.5, si_[:], MUL, MUL, [bh_, bi_], [B_YT[2 + hh]])

                for j in range(4):
                    bk, bb = nb()
                    for kc in range(8):
                        mm(bk[:, 0:256], hT[:, kc, j * 128:(j + 1) * 128], w_in[:, kc, 7 * 256:8 * 256],
                           kc == 0, kc == 7, [B_win[7]] + B_hT, [bb])
                    (s3, b3), (s4, b4) = S_(j % 2, 3), S_(j % 2, 4)
                    gelu2((bk[:, 0:256], bb), 256, s3[:, 0:256], b3, s4[:, 0:256], b4)
                    act(junk[:, 0:256], s4[:, 0:256], AF.Square, [b4], [B_junk, B_ssv], accum_out=ssv[:, j:j + 1])
                    ts("dve", rtv[:, j:j + 1], ssv[:, j:j + 1], 0.25 / 256, EPS, MUL, ADD, [B_ssv], [B_rtv])
                    ppow(rsv[:, j:j + 1], rtv[:, j:j + 1], 0, [B_rtv], [B_rsv])
                    stt("dve", vvt[j][:], s4[:, 0:256], rsv[:, j:j + 1], ggb[:], MUL, MUL,
                        [b4, B_rsv, B_ggb], [B_vvt[j]])
                for hh in range(2):
                    pu = inproj(6, hh)
                    pg = inproj(8, hh)
                    bks, bbs = nb()
                    for j in range(4):
                        for hl in range(2):
                            h = hh * 2 + hl
                            mm(bks[hl * 64:(hl + 1) * 64, j * 128:(j + 1) * 128],
                               vvt[j][:, h * 64:(h + 1) * 64], wst[:, h, :], True, True,
                               [B_vvt[j], B_wst], [bbs], skip=True)
                    (s0, b0), (s1, b1), (s2, b2) = S_(hh, 0), S_(hh, 1), S_(hh, 2)
                    silu2(s2[:], b2, pg)
                    gelu2((pu[0][:], pu[1]), G, s0[:], b0, s1[:], b1)
                    tt("dve", s0[:, :].rearrange("p (a b) -> p a b", b=128), bks[:, :].rearrange("p (a b) -> p a b", b=128),
                       bsb[:, hh, :].unsqueeze(1).broadcast_to([128, 4, 128]), ADD, [bbs, B_bsb], [b0])
                    tt("dve", s0[:], s0[:], s2[:], MUL, [b0, b2], [b0])
                    stt("dve", YT[4 + hh][:], s0[:], 0.25, s1[:], MUL, MUL, [b0, b1], [B_YT[4 + hh]])

                for hh in range(2):
                    if g > 0:
                        for c in range(4):
                            act(kwin[hh][:, c * 512:(c + 1) * 512], kwin[hh][:, (c + 1) * 512:(c + 2) * 512], AF.Copy,
                                [B_kwo[hh] if c < 3 else B_kwn[hh]], [B_kwo[hh]])
                            act(vwin[hh][:, c * 512:(c + 1) * 512], vwin[hh][:, (c + 1) * 512:(c + 2) * 512], AF.Copy,
                                [B_vwo[hh] if c < 3 else B_vwn[hh]], [B_vwo[hh]])
                    pz = {}
                    for nm, ch in (("q", 9), ("k", 10), ("v", 11), ("g", 12)):
                        bk, bb = nb()
                        ct = ch * 2 + hh
                        for kc in range(8):
                            mm(bk[:], w_in[:, kc, ct * 128:(ct + 1) * 128], hT[:, kc, :], kc == 0, kc == 7,
                               [B_win[ch]] + B_hT, [bb])
                        pz[nm] = (bk, bb)
                    act(Qbd[hh][0:64, 0, :], pz["q"][0][0:64, :], AF.Copy, [pz["q"][1]], [B_Qbd[hh]])
                    cp("dve", Qbd[hh][64:128, 1, :], pz["q"][0][64:128, :], [pz["q"][1]], [B_Qbd[hh]])
                    cp("dve", kwin[hh][:, 2048:2560], pz["k"][0][:], [pz["k"][1]], [B_kwn[hh]])
                    cp("dve", vwin[hh][:, 2048:2560], pz["v"][0][:], [pz["v"][1]], [B_vwn[hh]])
                    silu2(sgd[hh][:], B_sgd[hh], pz["g"])

                for hh in range(2):
                    acc = [(PS[2 * hh], B_PS[2 * hh]), (PS[2 * hh + 1], B_PS[2 * hh + 1])]
                    first = [True, True]
                    srr = [0]

                    def sbank():
                        i = 4 + (srr[0] % 3)
                        srr[0] += 1
                        return PS[i], B_PS[i]

                    prr = [0]
                    for pi, dil in ((2, 16), (1, 4), (0, 1)):
                        blocks = []
                        if dil == 1:
                            for b in range(-1, 4):
                                if b == -1 and g == 0:
                                    continue
                                blocks.append((b + 1, slice(2048 + 128 * b, 2048 + 128 * b + 128, 1), 128))
                        elif dil == 4:
                            for r in range(4):
                                if g > 0:
                                    blocks.append((r, slice(1536 + r, 2048, 4), 128))
                            for r in range(4):
                                blocks.append((4 + r, slice(2048 + r, 2560, 4), 128))
                        else:
                            for r in range(16):
                                if g > 0:
                                    blocks.append((r, slice(r, 2048, 16), 128))
                        i0 = 0
                        while i0 < len(blocks):
                            batch = [blocks[i0]]
                            i1 = i0 + 1
                            while (i1 < len(blocks) and len(batch) < 8 and blocks[i1][2] == batch[0][2]
                                   and blocks[i1][0] == batch[-1][0] + 1
                                   and (blocks[i1][1].start < 2048) == (batch[0][1].start < 2048)):
                                batch.append(blocks[i1])
                                i1 += 1
                            nk = batch[0][2]
                            for bi, (blk, sl, _) in enumerate(batch):
                                tr(PT[0:nk, bi * 128:(bi + 1) * 128], vwin[hh][:, sl], ident[:],
                                   [B_vwo[hh] if sl.start < 2048 else B_vwn[hh], B_ident], [B_PT])
                            b0 = batch[0][0]
                            nbk = len(batch)
                            ptv = PT[0:nk, 0:nbk * 128].rearrange("p (b c) -> p b c", b=nbk)
                            cp("act" if False else "dve", vaug[0:nk, b0:b0 + nbk, 0, 0:64], ptv[:, :, 0:64],
                               [B_PT], [B_VB[b] for b in range(b0, b0 + nbk)])
                            P.add("act", (lambda o, i: (lambda e: e.activation(out=o, in_=i, func=AF.Copy)))(
                                vaug[0:nk, b0:b0 + nbk, 1, 64:128], ptv[:, :, 64:128]),
                                [B_PT], [B_VB[b] for b in range(b0, b0 + nbk)], cost=220.0 + nbk * 64 / 1.4)
                            i0 = i1

                        nsub = 4 if dil < 16 else 16
                        wq = G // nsub
                        nloc = nsub // 2
                        hasA = g > 0 or dil == 1
                        nkB = 32 if dil == 16 else 128
                        K = kwin[hh]

                        def subinfo(s_):
                            if dil == 1:
                                return (slice(s_ * 128, (s_ + 1) * 128, 1),
                                        slice(2048 + 128 * (s_ - 1), 2048 + 128 * s_, 1),
                                        slice(2048 + 128 * s_, 2048 + 128 * (s_ + 1), 1),
                                        not (g == 0 and s_ == 0), s_, s_ + 1)
                            if dil == 4:
                                return (slice(s_, 512, 4), slice(1536 + s_, 2048, 4), slice(2048 + s_, 2560, 4),
                                        g > 0, s_, 4 + s_)
                            return (slice(s_, 512, 16), slice(s_, 2048, 16), slice(2048 + s_, 2560, 16),
                                    g > 0, s_, 16 + s_)

                        hasB = dil < 16
                        for bi in range(2):
                            subs = range(bi * nloc, (bi + 1) * nloc)
                            ptiles = {}
                            for ab in (0, 1):
                                if (ab == 0 and not hasA) or (ab == 1 and not hasB):
                                    continue
                                if ab == 0 and not any(subinfo(s_)[3] for s_ in subs):
                                    continue
                                bank, bbank = sbank()
                                npart = 128 if ab == 0 else nkB
                                for s_ in subs:
                                    qs, kA, kB, okA, blkA, blkB = subinfo(s_)
                                    sl = s_ - bi * nloc
                                    cs = slice(sl * 2 * wq, (sl + 1) * 2 * wq)
                                    rhs = Qbd[hh][:, :, qs]
                                    if ab == 0:
                                        if okA:
                                            mm(bank[:, cs], K[:, kA], rhs, True, True,
                                               [B_kwo[hh] if kA.start < 2048 else B_kwn[hh], B_Qbd[hh]], [bbank], skip=True)
                                    else:
                                        mm(bank[0:nkB, cs], K[:, kB], rhs, True, True,
                                           [B_kwn[hh], B_Qbd[hh]], [bbank], skip=True)
                                ti = (hh * 2) * 6 + pi * 2 + ab
                                k_ = prr[0] % NPT
                                prr[0] += 1
                                pt_, bpt_ = ptl[k_], B_ptl[k_]
                                c0 = 2 * wq if (ab == 0 and g == 0 and dil == 1 and bi == 0) else 0
                                act(pt_[0:npart, c0:G], bank[0:npart, c0:G], AF.Exp, [bbank], [bpt_], scale=0.125)
                                nl = (G - c0) // (2 * wq)
                                tt("dve", pt_[0:npart, c0:G].rearrange("p (a h b) -> p a h b", h=2, b=wq),
                                   pt_[0:npart, c0:G].rearrange("p (a h b) -> p a h b", h=2, b=wq),
                                   tab[0:npart, ti:ti + 7:6, 0:wq].unsqueeze(1).broadcast_to([npart, nl, 2, wq]), MUL,
                                   [bpt_, B_tab], [bpt_])
                                if ab == 0 and dil == 16 and 0 < g < 4:
                                    ms("pool", pt_[0:32 * (4 - g), :], 0.0, [bpt_])
                                ptiles[ab] = (pt_, bpt_)
                            for hl in range(2):
                                accb, accB = acc[hl]
                                for s_ in subs:
                                    qs, kA, kB, okA, blkA, blkB = subinfo(s_)
                                    sl = s_ - bi * nloc
                                    cs = slice(sl * 2 * wq + hl * wq, sl * 2 * wq + (hl + 1) * wq)
                                    if hasA and okA:
                                        pa_, bpa_ = ptiles[0]
                                        mm(accb[:, qs], vaug[:, blkA, hl, :], pa_[:, cs], first[hl], True,
                                           [B_VB[blkA], bpa_], [accB], skip=True)
                                        first[hl] = False
                                    if hasB:
                                        pb_, bpb_ = ptiles[1]
                                        mm(accb[:, qs], vaug[0:nkB, blkB, hl, :], pb_[0:nkB, cs], first[hl], True,
                                           [B_VB[blkB], bpb_], [accB], skip=True)
                                        first[hl] = False
                    sf, bfz = S_(hh, 4)
                    act(sf[0:64, :], acc[0][0][64:128, :], AF.Copy, [acc[0][1]], [bfz])
                    act(sf[64:128, :], acc[1][0][0:64, :], AF.Copy, [acc[1][1]], [bfz])
                    recip(sf[:], sf[:], [bfz], [bfz])
                    tt("dve", sf[:], sf[:], sgd[hh][:], MUL, [bfz, B_sgd[hh]], [bfz])
                    stt("dve", YT[6 + hh][0:64, :], acc[0][0][0:64, :], 0.5, sf[0:64, :], MUL, MUL,
                        [acc[0][1], bfz], [B_YT[6 + hh]])
                    stt("dve", YT[6 + hh][64:128, :], acc[1][0][64:128, :], 0.5, sf[64:128, :], MUL, MUL,
                        [acc[1][1], bfz], [B_YT[6 + hh]])

                for j in range(4):
                    T = g * 4 + j
                    xo = xb2[T % 2]
                    bxo = B_xb[T % 2]
                    dma(xo[:], src[T * 128:(T + 1) * 128, :], [B_XD[T]] if l > 0 else [], [bxo])
                    for nh in range(2):
                        bk, bb = nb()
                        for ct in range(8):
                            mm(bk[:], YT[ct][:, j * 128:(j + 1) * 128], w_out[:, ct, nh * 512:(nh + 1) * 512],
                               ct == 0, ct == 7, [B_YT[ct], B_wout], [bb])
                        tt("dve", xo[:, nh * 512:(nh + 1) * 512], bk[:], xo[:, nh * 512:(nh + 1) * 512], ADD,
                           [bb, bxo], [bxo])
                    if not last_layer:
                        dma(xs_d[T * 128:(T + 1) * 128, :], xo[:], [bxo], [B_XD[T]])
                    else:
                        c = 8 + T % 8
                        rmsnorm_stats(xo[:], bxo, c)
                        stt("dve", xo[:], xo[:], rs[:, c:c + 1], fgb[:], MUL, MUL,
                            [bxo, B_rs[c], B_fgb], [bxo])
                        dma(out_d[T * 128:(T + 1) * 128, :], xo[:], [bxo], [], is_out=True)
        P.emit(reorder=REORDER)
        build_nc.makespan = getattr(P, 'makespan', None)
    return nc


def _const_tables():
    slopes = [2.0 ** (-2.0 * (h + 1)) for h in range(4)]
    k = np.arange(128)[:, None]
    q = np.arange(128)[None, :]
    NEG = -240000.0
    tab = np.zeros((128, 24, 128), np.float32)
    for h in range(4):
        for pi, dil in enumerate((1, 4, 16)):
            dA = q + 128 - k
            tab[:, h * 6 + pi * 2 + 0, :] = np.where(k >= q, -8.0 * slopes[h] * dil * dA, NEG)
            dB = q - k
            tab[:, h * 6 + pi * 2 + 1, :] = np.where(q >= k, -8.0 * slopes[h] * dil * dB, NEG)
    cst = np.zeros((128, 2, 128), np.float32)
    cst[:, 0, :] = np.eye(128, dtype=np.float32)
    cst[:, 1, :] = (q >= k).astype(np.float32)
    tab = np.exp(tab / 8.0).astype(np.float32)
    both = ((q >= k) & ((q - k) % 4 == 0)).astype(np.float32)
    for h in range(4):
        tab[:, h * 6 + 3, :] *= (1.0 + both)
    return tab, cst


def _prep_shared(norm_g, w_in, conv_a_w, conv_r_w, conv_r_b, lru_wa, lru_ba, lru_wx, lru_bx,
                 lru_lambda, gmlp_norm_g, gmlp_ws, gmlp_bs, w_out, final_g):
    L = w_in.shape[0]
    f = np.float32
    pp = np.zeros((128, L, NPP), f)
    for hh in range(2):
        sl = slice(hh * 128, (hh + 1) * 128)
        for k in range(3):
            pp[:, :, hh * 3 + k] = conv_a_w[:, k, sl].T
        for k in range(4):
            pp[:, :, 6 + hh * 4 + k] = conv_r_w[:, k, sl].T
        pp[:, :, 14 + hh] = conv_r_b[:, sl].T
        pp[:, :, 16 + hh] = lru_ba[:, sl].T
        pp[:, :, 18 + hh] = lru_bx[:, sl].T
        pp[:, :, 20 + hh] = lru_lambda[:, sl].T
    wbd = np.zeros((L, 128, 4, 128), f)
    for gi, w in enumerate((lru_wa, lru_wx)):
        for h in range(4):
            hh, hl = h // 2, h % 2
            wbd[:, hl * 64:(hl + 1) * 64, gi * 2 + hh, hl * 64:(hl + 1) * 64] = w[:, h]
    wst = np.ascontiguousarray(np.transpose(gmlp_ws, (0, 3, 1, 2)))
    bsb = np.zeros((L, 128, 2, 512), f)
    for hh in range(2):
        for hl in range(2):
            bsb[:, hl * 64:(hl + 1) * 64, hh, :] = np.tile(gmlp_bs[:, hh * 2 + hl, :], (1, 4))[:, None, :]
    tab, cst = _const_tables()
    return {
        "w_in": np.ascontiguousarray(w_in, dtype=f),
        "w_out": np.ascontiguousarray(w_out, dtype=f),
        "ngb": np.ascontiguousarray(np.broadcast_to(norm_g[:, None, :], (L, 128, D)), dtype=f),
        "fgb": np.ascontiguousarray(np.broadcast_to(final_g[None, :], (128, D)), dtype=f),
        "pp": pp, "wbd": wbd,
        "ggb": np.ascontiguousarray(np.broadcast_to(gmlp_norm_g[:, None, :], (L, 128, 256)), dtype=f),
        "wst": wst, "bsb": bsb, "tab": tab, "cst": cst,
    }


_NC_CACHE = {}


def run(x, params, n_cores):
    x = np.asarray(x, np.float32)
    Bn, S, _ = x.shape
    L = params["w_in"].shape[0]
    shared = _prep_shared(**{k: np.asarray(v, np.float32) for k, v in params.items()})
    key = (L, S)
    if key not in _NC_CACHE:
        _NC_CACHE[key] = build_nc(L, S)
    nc = _NC_CACHE[key]
    in_maps = []
    for c in range(n_cores):
        m = dict(shared)
        m["x"] = np.ascontiguousarray(x[c % Bn])
        in_maps.append(m)
    res = run_bass_kernel_spmd(nc, in_maps, core_ids=list(range(n_cores)))
    return np.stack([res.results[b]["out"] for b in range(Bn)], axis=0).astype(np.float32)


def kernel(x, norm_g, w_in, conv_a_w, conv_r_w, conv_r_b, lru_wa, lru_ba, lru_wx, lru_bx,
           lru_lambda, gmlp_norm_g, gmlp_ws, gmlp_bs, w_out, final_g):
    params = dict(norm_g=norm_g, w_in=w_in, conv_a_w=conv_a_w, conv_r_w=conv_r_w, conv_r_b=conv_r_b,
                  lru_wa=lru_wa, lru_ba=lru_ba, lru_wx=lru_wx, lru_bx=lru_bx, lru_lambda=lru_lambda,
                  gmlp_norm_g=gmlp_norm_g, gmlp_ws=gmlp_ws, gmlp_bs=gmlp_bs, w_out=w_out, final_g=final_g)
    return run(x, params, 4)
```

```python
import numpy as np
from contextlib import ExitStack
import concourse.bass as bass
import concourse.mybir as mybir
from concourse.bass_utils import run_bass_kernel_spmd

F32 = mybir.dt.float32
BF16 = mybir.dt.bfloat16
AF = mybir.ActivationFunctionType
ALU = mybir.AluOpType

D = 1024
DIN = 3328
G = 512
NPP = 22
EPS = 1e-6
REORDER = True


class Buf:
    __slots__ = ("name", "w", "r")

    def __init__(self, name):
        self.name = name
        self.w = None
        self.r = []


class Op:
    __slots__ = ("eng", "fn", "deps", "is_dma", "needs_inc", "inc_idx",
                 "dma_sem", "dma_val", "clk", "cost", "lat", "idx", "t0", "t1", "succ", "npend", "prevdma")

    def __init__(self, eng, fn, is_dma):
        self.eng = eng
        self.fn = fn
        self.is_dma = is_dma
        self.deps = []
        self.needs_inc = False
        self.inc_idx = 0
        self.dma_sem = -1
        self.dma_val = 0
        self.clk = None
        self.succ = []
        self.prevdma = None


class Prog:
    ENGS = ("pe", "act", "dve", "pool", "sp")
    NDMA = {"sp": 16, "pool": 8, "act": 4}

    def __init__(self, nc):
        self.nc = nc
        self.all = []
        self.out_dmas = []
        self.last_dma = {}

    def add(self, eng, fn, reads=(), writes=(), dma=False, is_out=False, cost=200.0, lat=0.0):
        op = Op(eng, fn, dma)
        op.cost = cost
        op.lat = lat
        op.idx = len(self.all)
        deps = {}
        for b in reads:
            if b.w is not None:
                deps[id(b.w)] = (b.w, True)
        for b in writes:
            if b.w is not None and id(b.w) not in deps:
                deps[id(b.w)] = (b.w, False)
            for r in b.r:
                if id(r) not in deps:
                    deps[id(r)] = (r, False)
        for d, raw in deps.values():
            if d is op:
                continue
            op.deps.append((d, raw))
        for b in reads:
            b.r.append(op)
        for b in writes:
            b.w = op
            b.r = []
        self.all.append(op)
        if is_out:
            self.out_dmas.append(op)
        return op

    def schedule(self):
        import heapq
        ops = self.all
        for op in ops:
            op.npend = len(op.deps) + (1 if op.prevdma is not None else 0)
            for d, _ in op.deps:
                d.succ.append(op)
        nxt = {}
        for op in ops:
            if op.prevdma is not None:
                nxt[id(op.prevdma)] = op
        bl = {}
        for op in reversed(ops):
            m = 0.0
            for s2 in op.succ:
                v = bl[id(s2)]
                if v > m:
                    m = v
            bl[id(op)] = op.cost + op.lat + m
        for op in ops:
            op.idx = (-bl[id(op)], op.idx)
        ready = {e: [] for e in self.ENGS}
        rtime = {}
        for op in ops:
            if op.npend == 0:
                heapq.heappush(ready[op.eng], (op.idx, op))
                rtime[id(op)] = 0.0
        free = {e: 0.0 for e in self.ENGS}
        order = []
        nleft = len(ops)
        while nleft:
            best = None
            for e in self.ENGS:
                h = ready[e]
                if not h:
                    continue
                cand = heapq.nsmallest(6, h)
                pick = None
                for idx, op in cand:
                    if rtime[id(op)] <= free[e]:
                        pick = op
                        break
                if pick is None:
                    pick = min((c[1] for c in cand), key=lambda o: (rtime[id(o)], o.idx))
                st = max(free[e], rtime[id(pick)])
                if best is None or st < best[0] or (st == best[0] and pick.idx < best[2].idx):
                    best = (st, e, pick)
            st, e, op = best
            h = ready[e]
            h.remove((op.idx, op))
            heapq.heapify(h)
            op.t0 = st
            free[e] = st + op.cost
            op.t1 = st + op.cost + op.lat
            order.append(op)
            nleft -= 1
            rel = list(op.succ)
            for s2 in rel:
                s2.npend -= 1
                rtime[id(s2)] = max(rtime.get(id(s2), 0.0), op.t1)
                if s2.npend == 0:
                    heapq.heappush(ready[s2.eng], (s2.idx, s2))
            n2 = nxt.get(id(op))
            if n2 is not None:
                n2.npend -= 1
                rtime[id(n2)] = max(rtime.get(id(n2), 0.0), op.t0)
                if n2.npend == 0:
                    heapq.heappush(ready[n2.eng], (n2.idx, n2))
        self.makespan = max(o.t1 for o in order)
        return order

    def emit(self, reorder=True):
        nc = self.nc
        if reorder:
            order = self.schedule()
        else:
            order = list(self.all)
        self.ops = {e: [] for e in self.ENGS}
        for op in order:
            self.ops[op.eng].append(op)
        dma_rr = {e: 0 for e in self.ENGS}
        dma_cnt = {}
        dma_prev = {}
        sem_deps = {}
        for op in order:
            sd = []
            for d, raw in op.deps:
                if d.is_dma or d.eng != op.eng or op.eng != "pe":
                    sd.append(d)
                    d.needs_inc = True
            if op.is_dma:
                n = self.NDMA[op.eng]
                s = dma_rr[op.eng] % n
                dma_rr[op.eng] += 1
                key = (op.eng, s)
                prev = dma_prev.get(key)
                if prev is not None:
                    sd.append(prev)
                dma_cnt[key] = dma_cnt.get(key, 0) + 1
                op.dma_sem = key
                op.dma_val = 16 * dma_cnt[key]
                dma_prev[key] = op
            sem_deps[id(op)] = sd
        cnt = {e: 0 for e in self.ENGS}
        for e in self.ENGS:
            for op in self.ops[e]:
                if op.is_dma:
                    continue
                if op.needs_inc:
                    cnt[e] += 1
                op.inc_idx = cnt[e]
        known = {e: {} for e in self.ENGS}
        waits = {}
        for op in order:
            kn = known[op.eng]
            m = {}
            for d in sem_deps[id(op)]:
                if d.is_dma:
                    key = ("dma",) + d.dma_sem
                    val = d.dma_val
                else:
                    key = d.eng
                    val = d.inc_idx
                if kn.get(key, 0) >= val:
                    continue
                m[key] = max(m.get(key, 0), val)
                for k, v in d.clk.items():
                    if kn.get(k, 0) < v:
                        kn[k] = v
                if kn.get(key, 0) < val:
                    kn[key] = val
            waits[id(op)] = list(m.items())
            c = dict(kn)
            if not op.is_dma:
                c[op.eng] = max(c.get(op.eng, 0), op.inc_idx)
            op.clk = c

        with ExitStack() as st:
            sems = {e: st.enter_context(nc.semaphore("s_" + e)) for e in self.ENGS}
            dsems = {}
            for e, n in self.NDMA.items():
                for i in range(n):
                    dsems[(e, i)] = st.enter_context(nc.semaphore("d_%s%d" % (e, i)))
            block = st.enter_context(nc.Block())

            def body(ename):
                def f(eng):
                    for op in self.ops[ename]:
                        for key, val in waits[id(op)]:
                            if isinstance(key, tuple):
                                eng.wait_ge(dsems[key[1:]], val)
                            else:
                                eng.wait_ge(sems[key], val)
                        ins = op.fn(eng)
                        if op.is_dma:
                            ins.then_inc(dsems[op.dma_sem], 16)
                        elif op.needs_inc:
                            ins.then_inc(sems[ename], 1)
                    if ename == "sp":
                        last = {}
                        for op in self.out_dmas:
                            last[op.dma_sem] = max(last.get(op.dma_sem, 0), op.dma_val)
                        for s, v in last.items():
                            eng.wait_ge(dsems[s], v)
                return f

            block.tensor(body("pe"))
            block.scalar(body("act"))
            block.vector(body("dve"))
            block.gpsimd(body("pool"))
            block.sync(body("sp"))


def build_nc(L, S):
    NG = S // G
    nc = bass.Bass("TRN2", target_bir_lowering=False)
    dt = lambda n, s, d=F32, k="ExternalInput": nc.dram_tensor(n, s, d, kind=k).ap()
    x_d = dt("x", [S, D])
    win_d = dt("w_in", [L, D, DIN])
    wout_d = dt("w_out", [L, D, D])
    ngb_d = dt("ngb", [L, 128, D])
    fgb_d = dt("fgb", [128, D])
    pp_d = dt("pp", [128, L, NPP])
    wbd_d = dt("wbd", [L, 128, 4, 128])
    ggb_d = dt("ggb", [L, 128, 256])
    wst_d = dt("wst", [L, 128, 4, 128])
    bsb_d = dt("bsb", [L, 128, 2, 512])
    tab_d = dt("tab", [128, 24, 128])
    cst_d = dt("cst", [128, 2, 128])
    out_d = dt("out", [S, D], F32, "ExternalOutput")
    xs_d = dt("xs", [S, D], F32, "Internal")

    with ExitStack() as st:
        def sb(n, s, d=F32):
            return st.enter_context(nc.sbuf_tensor(n, s, d))

        def ps(n, s, d=F32):
            return st.enter_context(nc.psum_tensor(n, s, d))

        P = Prog(nc)

        w_in = sb("w_in_sb", [128, 8, DIN], BF16)
        w_out = sb("w_out_sb", [128, 8, D], BF16)
        xa = [sb("xa%d" % i, [128, D]) for i in range(2)]
        xb2 = [sb("xb%d" % i, [128, D]) for i in range(2)]
        ht = [sb("ht%d" % i, [128, D], BF16) for i in range(2)]
        hTs = [sb("hT%d" % i, [128, 8, G], BF16) for i in range(1)]
        ngb = sb("ngb_sb", [128, D])
        fgb = sb("fgb_sb", [128, D])
        pp = sb("pp_sb", [128, L, NPP])
        lc = sb("lc", [128, 2])
        lct = sb("lct", [128, 2])
        wbd = sb("wbd_sb", [128, 4, 128], BF16)
        ggb = sb("ggb_sb", [128, 256])
        wstf = sb("wstf", [128, 4, 128], BF16)
        wst = sb("wst_sb", [128, 4, 128], BF16)
        bsb = sb("bsb_sb", [128, 2, 128])
        tab = sb("tab_sb", [128, 24, 128], BF16)
        cstf = sb("cstf", [128, 2, 128])
        ident = sb("ident", [128, 128], BF16)
        ss = sb("ss", [128, 16])
        rt = sb("rt", [128, 16])
        rs = sb("rs", [128, 16])
        ssv = sb("ssv", [128, 4])
        rtv = sb("rtv", [128, 4])
        rsv = sb("rsv", [128, 4])
        NSC = 10
        SC = [sb("sc%d" % i, [128, G]) for i in range(NSC)]
        tA = [sb("tA%d" % i, [128, 2 + G]) for i in range(2)]
        xr = [sb("xr%d" % i, [128, 3 + G]) for i in range(2)]
        hst = sb("hst", [128, 2])
        xbbs = [sb("xbb%d" % i, [128, G], BF16) for i in range(2)]
        cpow = sb("cpow", [128, 2])
        lch = sb("lch", [128, 2])
        bh = sb("bh", [128, 4])
        vvt = [sb("vvt%d" % i, [128, 256], BF16) for i in range(4)]
        YTs = [[sb("yT%d_%d" % (p, i), [128, G], BF16) for i in range(8)] for p in range(2)]
        Qbd = [sb("Qbd%d" % i, [128, 2, G], BF16) for i in range(2)]
        sgd = [sb("sgd%d" % i, [128, G]) for i in range(2)]
        kwin = [sb("kwin%d" % i, [128, 2560], BF16) for i in range(2)]
        vwin = [sb("vwin%d" % i, [128, 2560], BF16) for i in range(2)]
        vaug = sb("vaug", [128, 16, 2, 128], BF16)
        NPT = 6
        ptl = [sb("pt%d" % i, [128, G], BF16) for i in range(NPT)]
        PS = [ps("ps%d" % i, [128, G]) for i in range(7)]
        PT = ps("pt", [128, 1024], BF16)

        bf = Buf
        B_win = [bf("w_in%d" % c) for c in range(13)]
        B_wout = bf("w_out")
        B_xa = [bf("xa0"), bf("xa1")]
        B_xb = [bf("xb0"), bf("xb1")]
        B_ht = [bf("ht0"), bf("ht1")]
        B_hTs = [[bf("hT%d_%d" % (p, i)) for i in range(4)] for p in range(1)]
        B_ngb = bf("ngb"); B_fgb = bf("fgb"); B_pp = bf("pp"); B_lc = bf("lc"); B_lct = bf("lct")
        B_wbd = bf("wbd"); B_ggb = bf("ggb"); B_wstf = bf("wstf"); B_wst = bf("wst"); B_bsb = bf("bsb")
        B_tab = bf("tab"); B_cstf = bf("cstf"); B_ident = bf("ident")
        junk = ht[1]; B_junk = B_ht[1]
        B_ss = [bf("ss%d" % i) for i in range(16)]; B_rt = [bf("rt%d" % i) for i in range(16)]; B_rs = [bf("rs%d" % i) for i in range(16)]
        B_ssv = bf("ssv"); B_rtv = bf("rtv"); B_rsv = bf("rsv")
        B_SC = [bf("sc%d" % i) for i in range(NSC)]
        B_tA = [bf("tA0"), bf("tA1")]
        B_xr = [bf("xr0"), bf("xr1")]
        B_hst = bf("hst"); B_xbbs = [bf("xbb0"), bf("xbb1")]; B_cpow = bf("cpow"); B_lch = bf("lch"); B_bh = bf("bh")
        B_vvt = [bf("vvt%d" % i) for i in range(4)]
        B_YTs = [[bf("yT%d_%d" % (p, i)) for i in range(8)] for p in range(2)]
        B_Qbd = [bf("Qbd0"), bf("Qbd1")]
        B_sgd = [bf("sgd0"), bf("sgd1")]
        B_kwo = [bf("kwo0"), bf("kwo1")]; B_kwn = [bf("kwn0"), bf("kwn1")]
        B_vwo = [bf("vwo0"), bf("vwo1")]; B_vwn = [bf("vwn0"), bf("vwn1")]
        B_VB = [bf("vb%d" % i) for i in range(16)]
        B_ptl = [bf("pt%d" % i) for i in range(6)]
        B_PS = [bf("ps%d" % i) for i in range(7)]
        B_PT = bf("pt")
        B_XD = [bf("xd%d" % t) for t in range(NG * 4)]

        def fs(ap):
            n = 1
            for d in ap.shape[1:]:
                n *= d
            return n

        def dma(out, in_, R, W, eng="sp", is_out=False):
            nbytes = fs(out) * out.shape[0] * 4
            P.add(eng, lambda e: e.dma_start(out=out, in_=in_), R, W, dma=True, is_out=is_out,
                  cost=(150.0 if eng == "sp" else 1500.0), lat=2500.0 + nbytes / 100.0)

        def act(out, in_, func, R, W, **kw):
            P.add("act", lambda e: e.activation(out=out, in_=in_, func=func, **kw), R, W,
                  cost=220.0 + fs(out) / 1.4, lat=250.0)

        def vcost(eng, out):
            if eng == "dve":
                return 120.0 + fs(out) / 0.96
            return 350.0 + 1.9 * fs(out)

        def tt(eng, out, in0, in1, op, R, W):
            P.add(eng, lambda e: e.tensor_tensor(out=out, in0=in0, in1=in1, op=op), R, W, cost=vcost(eng, out), lat=250.0)

        def ts(eng, out, in0, s1, s2, op0, op1, R, W):
            if s2 is None:
                P.add(eng, lambda e: e.tensor_scalar(out=out, in0=in0, scalar1=s1, scalar2=None, op0=op0), R, W,
                      cost=vcost(eng, out), lat=250.0)
            else:
                P.add(eng, lambda e: e.tensor_scalar(out=out, in0=in0, scalar1=s1, scalar2=s2, op0=op0, op1=op1),
                      R, W, cost=vcost(eng, out), lat=250.0)

        def stt(eng, out, in0, scalar, in1, op0, op1, R, W):
            P.add(eng, lambda e: e.scalar_tensor_tensor(out=out, in0=in0, scalar=scalar, in1=in1, op0=op0, op1=op1),
                  R, W, cost=vcost(eng, out), lat=250.0)

        def mm(out, lhsT, rhs, start, stop, R, W, skip=False):
            P.add("pe", lambda e: e.matmul(out, lhsT=lhsT, rhs=rhs, start=start, stop=stop,
                                           skip_group_check=skip), R, W, cost=60.0 + max(64, fs(rhs)) / 2.4, lat=250.0)

        def tr(out, in_, idn, R, W):
            P.add("pe", lambda e: e.transpose(out, in_, idn), R, W, cost=110.0, lat=250.0)

        def cp(eng, out, in_, R, W):
            P.add(eng, lambda e: e.tensor_copy(out=out, in_=in_), R, W, cost=vcost(eng, out), lat=250.0)

        def ms(eng, out, val, W):
            P.add(eng, lambda e: e.memset(out, val), (), W, cost=vcost(eng, out))

        def recip(out, in_, R, W):
            P.add("dve", lambda e: e.reciprocal(out=out, in_=in_), R, W, cost=150.0 + 6.3 * fs(out), lat=250.0)

        bank_rr = [0]

        def nb():
            i = bank_rr[0] % 7
            bank_rr[0] += 1
            return PS[i], B_PS[i]

        dma(cstf[:], cst_d[:, :, :], [], [B_cstf])
        cp("dve", ident[:], cstf[:, 0, :], [B_cstf], [B_ident])
        dma(tab[:], tab_d[:, :, :], [], [B_tab], eng="pool")
        dma(pp[:], pp_d[:, :, :], [], [B_pp])
        dma(fgb[:], fgb_d[:, :], [], [B_fgb])
        ms("pool", vaug[:], 1.0, B_VB)
        for hh in range(2):
            ms("pool", Qbd[hh][:], 0.0, [B_Qbd[hh]])
        ms("pool", cpow[:, 0:1], -0.5, [B_cpow])
        ms("pool", cpow[:, 1:2], 0.5, [B_cpow])

        MUL, ADD = ALU.mult, ALU.add

        def ppow(out, in_, which, R, W, n=1):
            ex = cpow[:, which:which + 1] if n == 1 else cpow[:, which:which + 1].broadcast_to([128, n])
            P.add("pool", lambda e: e.tensor_tensor(out=out, in0=in_, in1=ex, op=ALU.pow), R + [B_cpow], W,
                  cost=250.0 + n / 0.96)

        def rmsnorm_stats(xap, bx, c):
            act(junk[:], xap, AF.Square, [bx], [B_junk, B_ss[c]], accum_out=ss[:, c:c + 1])
            ts("dve", rt[:, c:c + 1], ss[:, c:c + 1], 1.0 / D, EPS, MUL, ADD, [B_ss[c]], [B_rt[c]])
            ppow(rs[:, c:c + 1], rt[:, c:c + 1], 0, [B_rt[c]], [B_rs[c]])

        for l in range(L):
            last_layer = (l == L - 1)
            for c in range(13):
                for kc in range(8):
                    dma(w_in[:, kc, c * 256:(c + 1) * 256],
                        win_d[l, kc * 128:(kc + 1) * 128, c * 256:(c + 1) * 256], [], [B_win[c]], eng="pool")
            for kc in range(8):
                dma(w_out[:, kc, :], wout_d[l, kc * 128:(kc + 1) * 128, :], [], [B_wout], eng="pool")
            dma(ngb[:], ngb_d[l, :, :], [], [B_ngb])
            dma(wbd[:], wbd_d[l, :, :, :], [], [B_wbd], eng="pool")
            dma(ggb[:], ggb_d[l, :, :], [], [B_ggb])
            dma(wstf[:], wst_d[l, :, :, :], [], [B_wstf], eng="pool")
            dma(bsb[:], bsb_d[l, :, :, 0:128], [], [B_bsb])
            for h in range(4):
                tt("dve", wst[:, h, :], wstf[:, h, :], cstf[:, 1, :], MUL, [B_wstf, B_cstf], [B_wst])
            act(lct[:], pp[:, l, 20:22], AF.Exp, [B_pp], [B_lct], scale=-1.0)
            act(lct[:], lct[:], AF.Ln, [B_lct], [B_lct], bias=1.0)
            ts("dve", lc[:], lct[:], -8.0, None, MUL, None, [B_lct], [B_lc])
            ts("dve", lch[:], lct[:], -4.0, None, MUL, None, [B_lct], [B_lch])
            ts("dve", bh[:], pp[:, l, 16:20], 0.5, None, MUL, None, [B_pp], [B_bh])
            ts("pool", ggb[:], ggb[:], 0.5, None, MUL, None, [B_ggb], [B_ggb])
            for hh in range(2):
                ms("pool", tA[hh][:, 0:2], 0.0, [B_tA[hh]])
                ms("pool", xr[hh][:, 0:3], 0.0, [B_xr[hh]])
                ms("pool", kwin[hh][:, 0:2048], 0.0, [B_kwo[hh]])
                ms("pool", vwin[hh][:, 0:2048], 0.0, [B_vwo[hh]])
            ms("pool", hst[:], 0.0, [B_hst])

            def PPc(col):
                return pp[:, l, col:col + 1]

            for g in range(NG):
                src = x_d if l == 0 else xs_d
                hT = hTs[0]
                YT = YTs[g % 2]
                B_YT = B_YTs[g % 2]
                B_hT = B_hTs[0]
                for j in range(4):
                    T = g * 4 + j
                    xb_ = xa[T % 2]
                    bx = B_xa[T % 2]
                    c = T % 8
                    dma(xb_[:], src[T * 128:(T + 1) * 128, :], [B_XD[T]] if l > 0 else [], [bx])
                    rmsnorm_stats(xb_[:], bx, c)
                    hb = ht[j % 2]
                    stt("dve", hb[:], xb_[:], rs[:, c:c + 1], ngb[:], MUL, MUL,
                        [bx, B_rs[c], B_ngb], [B_ht[j % 2]])
                    for kc in range(8):
                        tr(PT[:, kc * 128:(kc + 1) * 128], hb[:, kc * 128:(kc + 1) * 128], ident[:],
                           [B_ht[j % 2], B_ident], [B_PT])
                    P.add("act", (lambda jj, hTt: (lambda e: e.activation(
                        out=hTt[:, :, jj * 128:(jj + 1) * 128],
                        in_=PT[:, :].rearrange("p (k t) -> p k t", k=8), func=AF.Copy)))(j, hT),
                        [B_PT], [B_hT[j]], cost=220.0 + 1024 / 1.4)

                def S_(hh, t):
                    return SC[hh * 5 + t], B_SC[hh * 5 + t]

                def inproj(ch, hh):
                    bk, bb = nb()
                    ct = ch * 2 + hh
                    for kc in range(8):
                        mm(bk[:], w_in[:, kc, ct * 128:(ct + 1) * 128], hT[:, kc, :], kc == 0, kc == 7,
                           [B_win[ch]] + B_hT, [bb])
                    return bk, bb

                def silu2(dst, bdst, pg):
                    act(dst, pg[0][:], AF.Tanh, [pg[1]], [bdst], scale=0.5)
                    stt("dve", dst, dst, 1.0, pg[0][:], ADD, MUL, [bdst, pg[1]], [bdst])

                def gelu2(src, n, t_in, b_in, t_out, b_out):
                    act(t_in, src[0], AF.Square, [src[1]], [b_in])
                    ts("dve", t_in, t_in, 0.044715, 1.0, MUL, ADD, [b_in], [b_in])
                    tt("dve", t_in, t_in, src[0], MUL, [b_in, src[1]], [b_in])
                    act(t_out, t_in, AF.Tanh, [b_in], [b_out], scale=0.7978845608028654)
                    stt("dve", t_out, t_out, 1.0, src[0], ADD, MUL, [b_out, src[1]], [b_out])

                for hh in range(2):
                    px = inproj(0, hh)
                    pb = inproj(1, hh)
                    pc = inproj(2, hh)
                    pg = inproj(3, hh)
                    (s0, b0), (s1, b1) = S_(hh, 0), S_(hh, 1)
                    act(s0[:], px[0][:], AF.Copy, [px[1]], [b0])
                    tt("dve", tA[hh][:, 2:2 + G], pc[0][:], s0[:], MUL, [pc[1], b0], [B_tA[hh]])
                    silu2(s1[:], b1, pg)
                    ts("dve", s0[:], tA[hh][:, 2:2 + G], PPc(hh * 3 + 2), None, MUL, None, [B_tA[hh], B_pp], [b0])
                    stt("dve", s0[:], tA[hh][:, 1:1 + G], PPc(hh * 3 + 1), s0[:], MUL, ADD, [B_tA[hh], B_pp, b0], [b0])
                    stt("dve", s0[:], tA[hh][:, 0:G], PPc(hh * 3 + 0), s0[:], MUL, ADD, [B_tA[hh], B_pp, b0], [b0])
                    cp("pool", tA[hh][:, 0:2], tA[hh][:, G:G + 2], [B_tA[hh]], [B_tA[hh]])
                    tt("dve", s0[:], s0[:], pb[0][:], MUL, [b0, pb[1]], [b0])
                    stt("dve", YT[0 + hh][:], s0[:], 0.5, s1[:], MUL, MUL, [b0, b1], [B_YT[0 + hh]])

                for hh in range(2):
                    px = inproj(4, hh)
                    pg = inproj(5, hh)
                    (sc_, bc_), (sa_, ba_), (sm_, bm_), (si_, bi_), (sh_, bh_) = [S_(hh, t) for t in range(5)]
                    xbb = xbbs[hh]
                    B_xbb = B_xbbs[hh]
                    act(xr[hh][:, 3:3 + G], px[0][:], AF.Copy, [px[1]], [B_xr[hh]])
                    cb = 6 + hh * 4
                    ts("dve", sc_[:], xr[hh][:, 3:3 + G], PPc(cb + 3), PPc(14 + hh), MUL, ADD, [B_xr[hh], B_pp], [bc_])
                    for k in (2, 1, 0):
                        stt("dve", sc_[:], xr[hh][:, k:k + G], PPc(cb + k), sc_[:], MUL, ADD,
                            [B_xr[hh], B_pp, bc_], [bc_])
                    cp("pool", xr[hh][:, 0:3], xr[hh][:, G:G + 3], [B_xr[hh]], [B_xr[hh]])
                    act(xbb[:], sc_[:], AF.Copy, [bc_], [B_xbb])
                    bkr, bbr = nb()
                    mm(bkr[:], wbd[:, 0 + hh, :], xbb[:], True, True, [B_wbd, B_xbb], [bbr])
                    bki, bbi = nb()
                    mm(bki[:], wbd[:, 2 + hh, :], xbb[:], True, True, [B_wbd, B_xbb], [bbi])
                    act(sa_[:], bkr[:], AF.Tanh, [bbr, B_bh], [ba_], scale=0.5, bias=bh[:, hh:hh + 1])
                    act(si_[:], bki[:], AF.Tanh, [bbi, B_bh], [bi_], scale=0.5, bias=bh[:, 2 + hh:3 + hh])
                    act(sm_[:], sa_[:], AF.Exp, [ba_, B_lc], [bm_], scale=lc[:, hh:hh + 1], bias=lc[:, hh:hh + 1])
                    act(sa_[:], sa_[:], AF.Exp, [ba_, B_lch], [ba_], scale=lch[:, hh:hh + 1], bias=lch[:, hh:hh + 1])
                    ts("dve", sm_[:], sm_[:], 1.0, 1.0, ALU.min, ALU.subtract, [bm_], [bm_])
                    act(sm_[:], sm_[:], AF.Sqrt, [bm_], [bm_], scale=-1.0)
                    stt("dve", si_[:], si_[:], 1.0, sc_[:], ADD, MUL, [bi_, bc_], [bi_])
                    stt("dve", sm_[:], sm_[:], 0.5, si_[:], MUL, MUL, [bm_, bi_], [bm_])
                    P.add("dve", (lambda hh_, a_, m_, h_: (lambda e: e.tensor_tensor_scan(
                        out=h_[:], data0=a_[:], data1=m_[:], initial=hst[:, hh_:hh_ + 1],
                        op0=MUL, op1=ADD)))(hh, sa_, sm_, sh_),
                        [ba_, bm_, B_hst], [bh_], cost=120.0 + 2 * G / 0.96)
                    cp("pool", hst[:, hh:hh + 1], sh_[:, G - 1:G], [bh_], [B_hst])
                    silu2(si_[:], bi_, pg)
                    stt("dve", YT[2 + hh][:], sh_[:], 0.5, si_[:], MUL, MUL, [bh_, bi_], [B_YT[2 + hh]])

                for j in range(4):
                    bk, bb = nb()
                    for kc in range(8):
                        mm(bk[:, 0:256], hT[:, kc, j * 128:(j + 1) * 128], w_in[:, kc, 7 * 256:8 * 256],
                           kc == 0, kc == 7, [B_win[7]] + B_hT, [bb])
                    (s3, b3), (s4, b4) = S_(j % 2, 3), S_(j % 2, 4)
                    gelu2((bk[:, 0:256], bb), 256, s3[:, 0:256], b3, s4[:, 0:256], b4)
                    act(junk[:, 0:256], s4[:, 0:256], AF.Square, [b4], [B_junk, B_ssv], accum_out=ssv[:, j:j + 1])
                    ts("dve", rtv[:, j:j + 1], ssv[:, j:j + 1], 0.25 / 256, EPS, MUL, ADD, [B_ssv], [B_rtv])
                    ppow(rsv[:, j:j + 1], rtv[:, j:j + 1], 0, [B_rtv], [B_rsv])
                    stt("dve", vvt[j][:], s4[:, 0:256], rsv[:, j:j + 1], ggb[:], MUL, MUL,
                        [b4, B_rsv, B_ggb], [B_vvt[j]])
                for hh in range(2):
                    pu = inproj(6, hh)
                    pg = inproj(8, hh)
                    bks, bbs = nb()
                    for j in range(4):
                        for hl in range(2):
                            h = hh * 2 + hl
                            mm(bks[hl * 64:(hl + 1) * 64, j * 128:(j + 1) * 128],
                               vvt[j][:, h * 64:(h + 1) * 64], wst[:, h, :], True, True,
                               [B_vvt[j], B_wst], [bbs], skip=True)
                    (s0, b0), (s1, b1), (s2, b2) = S_(hh, 0), S_(hh, 1), S_(hh, 2)
                    silu2(s2[:], b2, pg)
                    gelu2((pu[0][:], pu[1]), G, s0[:], b0, s1[:], b1)
                    tt("dve", s0[:, :].rearrange("p (a b) -> p a b", b=128), bks[:, :].rearrange("p (a b) -> p a b", b=128),
                       bsb[:, hh, :].unsqueeze(1).broadcast_to([128, 4, 128]), ADD, [bbs, B_bsb], [b0])
                    tt("dve", s0[:], s0[:], s2[:], MUL, [b0, b2], [b0])
                    stt("dve", YT[4 + hh][:], s0[:], 0.25, s1[:], MUL, MUL, [b0, b1], [B_YT[4 + hh]])

                for hh in range(2):
                    if g > 0:
                        for c in range(4):
                            act(kwin[hh][:, c * 512:(c + 1) * 512], kwin[hh][:, (c + 1) * 512:(c + 2) * 512], AF.Copy,
                                [B_kwo[hh] if c < 3 else B_kwn[hh]], [B_kwo[hh]])
                            act(vwin[hh][:, c * 512:(c + 1) * 512], vwin[hh][:, (c + 1) * 512:(c + 2) * 512], AF.Copy,
                                [B_vwo[hh] if c < 3 else B_vwn[hh]], [B_vwo[hh]])
                    pz = {}
                    for nm, ch in (("q", 9), ("k", 10), ("v", 11), ("g", 12)):
                        bk, bb = nb()
                        ct = ch * 2 + hh
                        for kc in range(8):
                            mm(bk[:], w_in[:, kc, ct * 128:(ct + 1) * 128], hT[:, kc, :], kc == 0, kc == 7,
                               [B_win[ch]] + B_hT, [bb])
                        pz[nm] = (bk, bb)
                    act(Qbd[hh][0:64, 0, :], pz["q"][0][0:64, :], AF.Copy, [pz["q"][1]], [B_Qbd[hh]])
                    cp("dve", Qbd[hh][64:128, 1, :], pz["q"][0][64:128, :], [pz["q"][1]], [B_Qbd[hh]])
                    cp("dve", kwin[hh][:, 2048:2560], pz["k"][0][:], [pz["k"][1]], [B_kwn[hh]])
                    act(vwin[hh][:, 2048:2560], pz["v"][0][:], AF.Copy, [pz["v"][1]], [B_vwn[hh]])
                    silu2(sgd[hh][:], B_sgd[hh], pz["g"])

                for hh in range(2):
                    acc = [(PS[2 * hh], B_PS[2 * hh]), (PS[2 * hh + 1], B_PS[2 * hh + 1])]
                    first = [True, True]
                    srr = [0]

                    def sbank():
                        i = 4 + (srr[0] % 3)
                        srr[0] += 1
                        return PS[i], B_PS[i]

                    prr = [0]
                    for pi, dil in ((2, 16), (1, 4), (0, 1)):
                        blocks = []
                        if dil == 1:
                            for b in range(-1, 4):
                                if b == -1 and g == 0:
                                    continue
                                blocks.append((b + 1, slice(2048 + 128 * b, 2048 + 128 * b + 128, 1), 128))
                        elif dil == 4:
                            for r in range(4):
                                if g > 0:
                                    blocks.append((r, slice(1536 + r, 2048, 4), 128))
                            for r in range(4):
                                blocks.append((4 + r, slice(2048 + r, 2560, 4), 128))
                        else:
                            for r in range(16):
                                if g > 0:
                                    blocks.append((r, slice(r, 2048, 16), 128))
                        i0 = 0
                        while i0 < len(blocks):
                            batch = [blocks[i0]]
                            i1 = i0 + 1
                            while (i1 < len(blocks) and len(batch) < 8 and blocks[i1][2] == batch[0][2]
                                   and blocks[i1][0] == batch[-1][0] + 1
                                   and (blocks[i1][1].start < 2048) == (batch[0][1].start < 2048)):
                                batch.append(blocks[i1])
                                i1 += 1
                            nk = batch[0][2]
                            for bi, (blk, sl, _) in enumerate(batch):
                                tr(PT[0:nk, bi * 128:(bi + 1) * 128], vwin[hh][:, sl], ident[:],
                                   [B_vwo[hh] if sl.start < 2048 else B_vwn[hh], B_ident], [B_PT])
                            b0 = batch[0][0]
                            nbk = len(batch)
                            ptv = PT[0:nk, 0:nbk * 128].rearrange("p (b c) -> p b c", b=nbk)
                            cp("act" if False else "dve", vaug[0:nk, b0:b0 + nbk, 0, 0:64], ptv[:, :, 0:64],
                               [B_PT], [B_VB[b] for b in range(b0, b0 + nbk)])
                            P.add("act", (lambda o, i: (lambda e: e.activation(out=o, in_=i, func=AF.Copy)))(
                                vaug[0:nk, b0:b0 + nbk, 1, 64:128], ptv[:, :, 64:128]),
                                [B_PT], [B_VB[b] for b in range(b0, b0 + nbk)], cost=220.0 + nbk * 64 / 1.4)
                            i0 = i1

                        nsub = 4 if dil < 16 else 16
                        wq = G // nsub
                        nloc = nsub // 2
                        hasA = g > 0 or dil == 1
                        nkB = 32 if dil == 16 else 128
                        K = kwin[hh]

                        def subinfo(s_):
                            if dil == 1:
                                return (slice(s_ * 128, (s_ + 1) * 128, 1),
                                        slice(2048 + 128 * (s_ - 1), 2048 + 128 * s_, 1),
                                        slice(2048 + 128 * s_, 2048 + 128 * (s_ + 1), 1),
                                        not (g == 0 and s_ == 0), s_, s_ + 1)
                            if dil == 4:
                                return (slice(s_, 512, 4), slice(1536 + s_, 2048, 4), slice(2048 + s_, 2560, 4),
                                        g > 0, s_, 4 + s_)
                            return (slice(s_, 512, 16), slice(s_, 2048, 16), slice(2048 + s_, 2560, 16),
                                    g > 0, s_, 16 + s_)

                        hasB = dil < 16
                        for bi in range(2):
                            subs = range(bi * nloc, (bi + 1) * nloc)
                            ptiles = {}
                            for ab in (0, 1):
                                if (ab == 0 and not hasA) or (ab == 1 and not hasB):
                                    continue
                                if ab == 0 and not any(subinfo(s_)[3] for s_ in subs):
                                    continue
                                bank, bbank = sbank()
                                npart = 128 if ab == 0 else nkB
                                for s_ in subs:
                                    qs, kA, kB, okA, blkA, blkB = subinfo(s_)
                                    sl = s_ - bi * nloc
                                    cs = slice(sl * 2 * wq, (sl + 1) * 2 * wq)
                                    rhs = Qbd[hh][:, :, qs]
                                    if ab == 0:
                                        if okA:
                                            mm(bank[:, cs], K[:, kA], rhs, True, True,
                                               [B_kwo[hh] if kA.start < 2048 else B_kwn[hh], B_Qbd[hh]], [bbank], skip=True)
                                    else:
                                        mm(bank[0:nkB, cs], K[:, kB], rhs, True, True,
                                           [B_kwn[hh], B_Qbd[hh]], [bbank], skip=True)
                                ti = (hh * 2) * 6 + pi * 2 + ab
                                k_ = prr[0] % NPT
                                prr[0] += 1
                                pt_, bpt_ = ptl[k_], B_ptl[k_]
                                c0 = 2 * wq if (ab == 0 and g == 0 and dil == 1 and bi == 0) else 0
                                act(pt_[0:npart, c0:G], bank[0:npart, c0:G], AF.Exp, [bbank], [bpt_], scale=0.125)
                                nl = (G - c0) // (2 * wq)
                                tt("dve", pt_[0:npart, c0:G].rearrange("p (a h b) -> p a h b", h=2, b=wq),
                                   pt_[0:npart, c0:G].rearrange("p (a h b) -> p a h b", h=2, b=wq),
                                   tab[0:npart, ti:ti + 7:6, 0:wq].unsqueeze(1).broadcast_to([npart, nl, 2, wq]), MUL,
                                   [bpt_, B_tab], [bpt_])
                                if ab == 0 and dil == 16 and 0 < g < 4:
                                    ms("pool", pt_[0:32 * (4 - g), :], 0.0, [bpt_])
                                ptiles[ab] = (pt_, bpt_)
                            for hl in range(2):
                                accb, accB = acc[hl]
                                for s_ in subs:
                                    qs, kA, kB, okA, blkA, blkB = subinfo(s_)
                                    sl = s_ - bi * nloc
                                    cs = slice(sl * 2 * wq + hl * wq, sl * 2 * wq + (hl + 1) * wq)
                                    if hasA and okA:
                                        pa_, bpa_ = ptiles[0]
                                        mm(accb[:, qs], vaug[:, blkA, hl, :], pa_[:, cs], first[hl], True,
                                           [B_VB[blkA], bpa_], [accB], skip=True)
                                        first[hl] = False
                                    if hasB:
                                        pb_, bpb_ = ptiles[1]
                                        mm(accb[:, qs], vaug[0:nkB, blkB, hl, :], pb_[0:nkB, cs], first[hl], True,
                                           [B_VB[blkB], bpb_], [accB], skip=True)
                                        first[hl] = False
                    sf, bfz = S_(hh, 4)
                    act(sf[0:64, :], acc[0][0][64:128, :], AF.Copy, [acc[0][1]], [bfz])
                    act(sf[64:128, :], acc[1][0][0:64, :], AF.Copy, [acc[1][1]], [bfz])
                    recip(sf[:], sf[:], [bfz], [bfz])
                    tt("dve", sf[:], sf[:], sgd[hh][:], MUL, [bfz, B_sgd[hh]], [bfz])
                    stt("dve", YT[6 + hh][0:64, :], acc[0][0][0:64, :], 0.5, sf[0:64, :], MUL, MUL,
                        [acc[0][1], bfz], [B_YT[6 + hh]])
                    stt("dve", YT[6 + hh][64:128, :], acc[1][0][64:128, :], 0.5, sf[64:128, :], MUL, MUL,
                        [acc[1][1], bfz], [B_YT[6 + hh]])

                for j in range(4):
                    T = g * 4 + j
                    xo = xb2[T % 2]
                    bxo = B_xb[T % 2]
                    dma(xo[:], src[T * 128:(T + 1) * 128, :], [B_XD[T]] if l > 0 else [], [bxo])
                    for nh in range(2):
                        bk, bb = nb()
                        for ct in range(8):
                            mm(bk[:], YT[ct][:, j * 128:(j + 1) * 128], w_out[:, ct, nh * 512:(nh + 1) * 512],
                               ct == 0, ct == 7, [B_YT[ct], B_wout], [bb])
                        tt("dve", xo[:, nh * 512:(nh + 1) * 512], bk[:], xo[:, nh * 512:(nh + 1) * 512], ADD,
                           [bb, bxo], [bxo])
                    if not last_layer:
                        dma(xs_d[T * 128:(T + 1) * 128, :], xo[:], [bxo], [B_XD[T]])
                    else:
                        c = 8 + T % 8
                        rmsnorm_stats(xo[:], bxo, c)
                        stt("dve", xo[:], xo[:], rs[:, c:c + 1], fgb[:], MUL, MUL,
                            [bxo, B_rs[c], B_fgb], [bxo])
                        dma(out_d[T * 128:(T + 1) * 128, :], xo[:], [bxo], [], is_out=True)
        P.emit(reorder=REORDER)
        build_nc.makespan = getattr(P, 'makespan', None)
    return nc


def _const_tables():
    slopes = [2.0 ** (-2.0 * (h + 1)) for h in range(4)]
    k = np.arange(128)[:, None]
    q = np.arange(128)[None, :]
    NEG = -240000.0
    tab = np.zeros((128, 24, 128), np.float32)
    for h in range(4):
        for pi, dil in enumerate((1, 4, 16)):
            dA = q + 128 - k
            tab[:, h * 6 + pi * 2 + 0, :] = np.where(k >= q, -8.0 * slopes[h] * dil * dA, NEG)
            dB = q - k
            tab[:, h * 6 + pi * 2 + 1, :] = np.where(q >= k, -8.0 * slopes[h] * dil * dB, NEG)
    cst = np.zeros((128, 2, 128), np.float32)
    cst[:, 0, :] = np.eye(128, dtype=np.float32)
    cst[:, 1, :] = (q >= k).astype(np.float32)
    tab = np.exp(tab / 8.0).astype(np.float32)
    both = ((q >= k) & ((q - k) % 4 == 0)).astype(np.float32)
    for h in range(4):
        tab[:, h * 6 + 3, :] *= (1.0 + both)
    return tab, cst


def _prep_shared(norm_g, w_in, conv_a_w, conv_r_w, conv_r_b, lru_wa, lru_ba, lru_wx, lru_bx,
                 lru_lambda, gmlp_norm_g, gmlp_ws, gmlp_bs, w_out, final_g):
    L = w_in.shape[0]
    f = np.float32
    pp = np.zeros((128, L, NPP), f)
    for hh in range(2):
        sl = slice(hh * 128, (hh + 1) * 128)
        for k in range(3):
            pp[:, :, hh * 3 + k] = conv_a_w[:, k, sl].T
        for k in range(4):
            pp[:, :, 6 + hh * 4 + k] = conv_r_w[:, k, sl].T
        pp[:, :, 14 + hh] = conv_r_b[:, sl].T
        pp[:, :, 16 + hh] = lru_ba[:, sl].T
        pp[:, :, 18 + hh] = lru_bx[:, sl].T
        pp[:, :, 20 + hh] = lru_lambda[:, sl].T
    wbd = np.zeros((L, 128, 4, 128), f)
    for gi, w in enumerate((lru_wa, lru_wx)):
        for h in range(4):
            hh, hl = h // 2, h % 2
            wbd[:, hl * 64:(hl + 1) * 64, gi * 2 + hh, hl * 64:(hl + 1) * 64] = w[:, h]
    wst = np.ascontiguousarray(np.transpose(gmlp_ws, (0, 3, 1, 2)))
    bsb = np.zeros((L, 128, 2, 512), f)
    for hh in range(2):
        for hl in range(2):
            bsb[:, hl * 64:(hl + 1) * 64, hh, :] = np.tile(gmlp_bs[:, hh * 2 + hl, :], (1, 4))[:, None, :]
    tab, cst = _const_tables()
    return {
        "w_in": np.ascontiguousarray(w_in, dtype=f),
        "w_out": np.ascontiguousarray(w_out, dtype=f),
        "ngb": np.ascontiguousarray(np.broadcast_to(norm_g[:, None, :], (L, 128, D)), dtype=f),
        "fgb": np.ascontiguousarray(np.broadcast_to(final_g[None, :], (128, D)), dtype=f),
        "pp": pp, "wbd": wbd,
        "ggb": np.ascontiguousarray(np.broadcast_to(gmlp_norm_g[:, None, :], (L, 128, 256)), dtype=f),
        "wst": wst, "bsb": bsb, "tab": tab, "cst": cst,
    }


_NC_CACHE = {}


def run(x, params, n_cores):
    x = np.asarray(x, np.float32)
    Bn, S, _ = x.shape
    L = params["w_in"].shape[0]
    shared = _prep_shared(**{k: np.asarray(v, np.float32) for k, v in params.items()})
    key = (L, S)
    if key not in _NC_CACHE:
        _NC_CACHE[key] = build_nc(L, S)
    nc = _NC_CACHE[key]
    in_maps = []
    for c in range(n_cores):
        m = dict(shared)
        m["x"] = np.ascontiguousarray(x[c % Bn])
        in_maps.append(m)
    res = run_bass_kernel_spmd(nc, in_maps, core_ids=list(range(n_cores)))
    return np.stack([res.results[b]["out"] for b in range(Bn)], axis=0).astype(np.float32)


def kernel(x, norm_g, w_in, conv_a_w, conv_r_w, conv_r_b, lru_wa, lru_ba, lru_wx, lru_bx,
           lru_lambda, gmlp_norm_g, gmlp_ws, gmlp_bs, w_out, final_g):
    params = dict(norm_g=norm_g, w_in=w_in, conv_a_w=conv_a_w, conv_r_w=conv_r_w, conv_r_b=conv_r_b,
                  lru_wa=lru_wa, lru_ba=lru_ba, lru_wx=lru_wx, lru_bx=lru_bx, lru_lambda=lru_lambda,
                  gmlp_norm_g=gmlp_norm_g, gmlp_ws=gmlp_ws, gmlp_bs=gmlp_bs, w_out=w_out, final_g=final_g)
    return run(x, params, 4)
```

```python
import numpy as np
from contextlib import ExitStack
import concourse.bass as bass
import concourse.mybir as mybir
from concourse.bass_utils import run_bass_kernel_spmd

F32 = mybir.dt.float32
BF16 = mybir.dt.bfloat16
AF = mybir.ActivationFunctionType
ALU = mybir.AluOpType

D = 1024
DIN = 3328
G = 512
NPP = 22
EPS = 1e-6
REORDER = True


class Buf:
    __slots__ = ("name", "w", "r")

    def __init__(self, name):
        self.name = name
        self.w = None
        self.r = []


class Op:
    __slots__ = ("eng", "fn", "deps", "is_dma", "needs_inc", "inc_idx",
                 "dma_sem", "dma_val", "clk", "cost", "lat", "idx", "t0", "t1", "succ", "npend", "prevdma")

    def __init__(self, eng, fn, is_dma):
        self.eng = eng
        self.fn = fn
        self.is_dma = is_dma
        self.deps = []
        self.needs_inc = False
        self.inc_idx = 0
        self.dma_sem = -1
        self.dma_val = 0
        self.clk = None
        self.succ = []
        self.prevdma = None


class Prog:
    ENGS = ("pe", "act", "dve", "pool", "sp")
    NDMA = {"sp": 16, "pool": 8, "act": 4}

    def __init__(self, nc):
        self.nc = nc
        self.all = []
        self.out_dmas = []
        self.last_dma = {}

    def add(self, eng, fn, reads=(), writes=(), dma=False, is_out=False, cost=200.0, lat=0.0):
        op = Op(eng, fn, dma)
        op.cost = cost
        op.lat = lat
        op.idx = len(self.all)
        deps = {}
        for b in reads:
            if b.w is not None:
                deps[id(b.w)] = (b.w, True)
        for b in writes:
            if b.w is not None and id(b.w) not in deps:
                deps[id(b.w)] = (b.w, False)
            for r in b.r:
                if id(r) not in deps:
                    deps[id(r)] = (r, False)
        for d, raw in deps.values():
            if d is op:
                continue
            op.deps.append((d, raw))
        for b in reads:
            b.r.append(op)
        for b in writes:
            b.w = op
            b.r = []
        self.all.append(op)
        if is_out:
            self.out_dmas.append(op)
        return op

    def schedule(self):
        import heapq
        ops = self.all
        for op in ops:
            op.npend = len(op.deps) + (1 if op.prevdma is not None else 0)
            for d, _ in op.deps:
                d.succ.append(op)
        nxt = {}
        for op in ops:
            if op.prevdma is not None:
                nxt[id(op.prevdma)] = op
        bl = {}
        for op in reversed(ops):
            m = 0.0
            for s2 in op.succ:
                v = bl[id(s2)]
                if v > m:
                    m = v
            bl[id(op)] = op.cost + op.lat + m
        for op in ops:
            op.idx = (-bl[id(op)], op.idx)
        ready = {e: [] for e in self.ENGS}
        rtime = {}
        for op in ops:
            if op.npend == 0:
                heapq.heappush(ready[op.eng], (op.idx, op))
                rtime[id(op)] = 0.0
        free = {e: 0.0 for e in self.ENGS}
        order = []
        nleft = len(ops)
        while nleft:
            best = None
            for e in self.ENGS:
                h = ready[e]
                if not h:
                    continue
                cand = heapq.nsmallest(6, h)
                pick = None
                for idx, op in cand:
                    if rtime[id(op)] <= free[e]:
                        pick = op
                        break
                if pick is None:
                    pick = min((c[1] for c in cand), key=lambda o: (rtime[id(o)], o.idx))
                st = max(free[e], rtime[id(pick)])
                if best is None or st < best[0] or (st == best[0] and pick.idx < best[2].idx):
                    best = (st, e, pick)
            st, e, op = best
            h = ready[e]
            h.remove((op.idx, op))
            heapq.heapify(h)
            op.t0 = st
            free[e] = st + op.cost
            op.t1 = st + op.cost + op.lat
            order.append(op)
            nleft -= 1
            rel = list(op.succ)
            for s2 in rel:
                s2.npend -= 1
                rtime[id(s2)] = max(rtime.get(id(s2), 0.0), op.t1)
                if s2.npend == 0:
                    heapq.heappush(ready[s2.eng], (s2.idx, s2))
            n2 = nxt.get(id(op))
            if n2 is not None:
                n2.npend -= 1
                rtime[id(n2)] = max(rtime.get(id(n2), 0.0), op.t0)
                if n2.npend == 0:
                    heapq.heappush(ready[n2.eng], (n2.idx, n2))
        self.makespan = max(o.t1 for o in order)
        return order

    def emit(self, reorder=True):
        nc = self.nc
        if reorder:
            order = self.schedule()
        else:
            order = list(self.all)
        self.ops = {e: [] for e in self.ENGS}
        for op in order:
            self.ops[op.eng].append(op)
        dma_rr = {e: 0 for e in self.ENGS}
        dma_cnt = {}
        dma_prev = {}
        sem_deps = {}
        for op in order:
            sd = []
            for d, raw in op.deps:
                if d.is_dma or d.eng != op.eng or op.eng != "pe":
                    sd.append(d)
                    d.needs_inc = True
            if op.is_dma:
                n = self.NDMA[op.eng]
                s = dma_rr[op.eng] % n
                dma_rr[op.eng] += 1
                key = (op.eng, s)
                prev = dma_prev.get(key)
                if prev is not None:
                    sd.append(prev)
                dma_cnt[key] = dma_cnt.get(key, 0) + 1
                op.dma_sem = key
                op.dma_val = 16 * dma_cnt[key]
                dma_prev[key] = op
            sem_deps[id(op)] = sd
        cnt = {e: 0 for e in self.ENGS}
        for e in self.ENGS:
            for op in self.ops[e]:
                if op.is_dma:
                    continue
                if op.needs_inc:
                    cnt[e] += 1
                op.inc_idx = cnt[e]
        known = {e: {} for e in self.ENGS}
        waits = {}
        for op in order:
            kn = known[op.eng]
            m = {}
            for d in sem_deps[id(op)]:
                if d.is_dma:
                    key = ("dma",) + d.dma_sem
                    val = d.dma_val
                else:
                    key = d.eng
                    val = d.inc_idx
                if kn.get(key, 0) >= val:
                    continue
                m[key] = max(m.get(key, 0), val)
                for k, v in d.clk.items():
                    if kn.get(k, 0) < v:
                        kn[k] = v
                if kn.get(key, 0) < val:
                    kn[key] = val
            waits[id(op)] = list(m.items())
            c = dict(kn)
            if not op.is_dma:
                c[op.eng] = max(c.get(op.eng, 0), op.inc_idx)
            op.clk = c

        with ExitStack() as st:
            sems = {e: st.enter_context(nc.semaphore("s_" + e)) for e in self.ENGS}
            dsems = {}
            for e, n in self.NDMA.items():
                for i in range(n):
                    dsems[(e, i)] = st.enter_context(nc.semaphore("d_%s%d" % (e, i)))
            block = st.enter_context(nc.Block())

            def body(ename):
                def f(eng):
                    for op in self.ops[ename]:
                        for key, val in waits[id(op)]:
                            if isinstance(key, tuple):
                                eng.wait_ge(dsems[key[1:]], val)
                            else:
                                eng.wait_ge(sems[key], val)
                        ins = op.fn(eng)
                        if op.is_dma:
                            ins.then_inc(dsems[op.dma_sem], 16)
                        elif op.needs_inc:
                            ins.then_inc(sems[ename], 1)
                    if ename == "sp":
                        last = {}
                        for op in self.out_dmas:
                            last[op.dma_sem] = max(last.get(op.dma_sem, 0), op.dma_val)
                        for s, v in last.items():
                            eng.wait_ge(dsems[s], v)
                return f

            block.tensor(body("pe"))
            block.scalar(body("act"))
            block.vector(body("dve"))
            block.gpsimd(body("pool"))
            block.sync(body("sp"))


def build_nc(L, S):
    NG = S // G
    nc = bass.Bass("TRN2", target_bir_lowering=False)
    dt = lambda n, s, d=F32, k="ExternalInput": nc.dram_tensor(n, s, d, kind=k).ap()
    x_d = dt("x", [S, D])
    win_d = dt("w_in", [L, D, DIN])
    wout_d = dt("w_out", [L, D, D])
    ngb_d = dt("ngb", [L, 128, D])
    fgb_d = dt("fgb", [128, D])
    pp_d = dt("pp", [128, L, NPP])
    wbd_d = dt("wbd", [L, 128, 4, 128])
    ggb_d = dt("ggb", [L, 128, 256])
    wst_d = dt("wst", [L, 128, 4, 128])
    bsb_d = dt("bsb", [L, 128, 2, 512])
    tab_d = dt("tab", [128, 24, 128])
    cst_d = dt("cst", [128, 2, 128])
    out_d = dt("out", [S, D], F32, "ExternalOutput")
    xs_d = dt("xs", [S, D], F32, "Internal")

    with ExitStack() as st:
        def sb(n, s, d=F32):
            return st.enter_context(nc.sbuf_tensor(n, s, d))

        def ps(n, s, d=F32):
            return st.enter_context(nc.psum_tensor(n, s, d))

        P = Prog(nc)

        w_in = sb("w_in_sb", [128, 8, DIN], BF16)
        w_out = sb("w_out_sb", [128, 8, D], BF16)
        xa = [sb("xa%d" % i, [128, D]) for i in range(2)]
        xb2 = [sb("xb%d" % i, [128, D]) for i in range(2)]
        ht = [sb("ht%d" % i, [128, D], BF16) for i in range(2)]
        hTs = [sb("hT%d" % i, [128, 8, G], BF16) for i in range(1)]
        ngb = sb("ngb_sb", [128, D])
        fgb = sb("fgb_sb", [128, D])
        pp = sb("pp_sb", [128, L, NPP])
        lc = sb("lc", [128, 2])
        lct = sb("lct", [128, 2])
        wbd = sb("wbd_sb", [128, 4, 128], BF16)
        ggb = sb("ggb_sb", [128, 256])
        wstf = sb("wstf", [128, 4, 128], BF16)
        wst = sb("wst_sb", [128, 4, 128], BF16)
        bsb = sb("bsb_sb", [128, 2, 128])
        tab = sb("tab_sb", [128, 24, 128], BF16)
        cstf = sb("cstf", [128, 2, 128])
        ident = sb("ident", [128, 128], BF16)
        ss = sb("ss", [128, 16])
        rt = sb("rt", [128, 16])
        rs = sb("rs", [128, 16])
        ssv = sb("ssv", [128, 4])
        rtv = sb("rtv", [128, 4])
        rsv = sb("rsv", [128, 4])
        NSC = 10
        SC = [sb("sc%d" % i, [128, G]) for i in range(NSC)]
        tA = [sb("tA%d" % i, [128, 2 + G]) for i in range(2)]
        xr = [sb("xr%d" % i, [128, 3 + G]) for i in range(2)]
        hst = sb("hst", [128, 2])
        xbbs = [sb("xbb%d" % i, [128, G], BF16) for i in range(2)]
        cpow = sb("cpow", [128, 2])
        lch = sb("lch", [128, 2])
        bh = sb("bh", [128, 4])
        vvt = [sb("vvt%d" % i, [128, 256], BF16) for i in range(4)]
        YTs = [[sb("yT%d_%d" % (p, i), [128, G], BF16) for i in range(8)] for p in range(2)]
        Qbd = [sb("Qbd%d" % i, [128, 2, G], BF16) for i in range(2)]
        sgd = [sb("sgd%d" % i, [128, G]) for i in range(2)]
        kwin = [sb("kwin%d" % i, [128, 2560], BF16) for i in range(2)]
        vwin = [sb("vwin%d" % i, [128, 2560], BF16) for i in range(2)]
        vaug = sb("vaug", [128, 16, 2, 128], BF16)
        NPT = 6
        ptl = [sb("pt%d" % i, [128, G], BF16) for i in range(NPT)]
        PS = [ps("ps%d" % i, [128, G]) for i in range(7)]
        PT = ps("pt", [128, 1024], BF16)

        bf = Buf
        B_win = [bf("w_in%d" % c) for c in range(13)]
        B_wout = bf("w_out")
        B_xa = [bf("xa0"), bf("xa1")]
        B_xb = [bf("xb0"), bf("xb1")]
        B_ht = [bf("ht0"), bf("ht1")]
        B_hTs = [[bf("hT%d_%d" % (p, i)) for i in range(4)] for p in range(1)]
        B_ngb = bf("ngb"); B_fgb = bf("fgb"); B_pp = bf("pp"); B_lc = bf("lc"); B_lct = bf("lct")
        B_wbd = bf("wbd"); B_ggb = bf("ggb"); B_wstf = bf("wstf"); B_wst = bf("wst"); B_bsb = bf("bsb")
        B_tab = bf("tab"); B_cstf = bf("cstf"); B_ident = bf("ident")
        junk = ht[1]; B_junk = B_ht[1]
        B_ss = [bf("ss%d" % i) for i in range(16)]; B_rt = [bf("rt%d" % i) for i in range(16)]; B_rs = [bf("rs%d" % i) for i in range(16)]
        B_ssv = bf("ssv"); B_rtv = bf("rtv"); B_rsv = bf("rsv")
        B_SC = [bf("sc%d" % i) for i in range(NSC)]
        B_tA = [bf("tA0"), bf("tA1")]
        B_xr = [bf("xr0"), bf("xr1")]
        B_hst = bf("hst"); B_xbbs = [bf("xbb0"), bf("xbb1")]; B_cpow = bf("cpow"); B_lch = bf("lch"); B_bh = bf("bh")
        B_vvt = [bf("vvt%d" % i) for i in range(4)]
        B_YTs = [[bf("yT%d_%d" % (p, i)) for i in range(8)] for p in range(2)]
        B_Qbd = [bf("Qbd0"), bf("Qbd1")]
        B_sgd = [bf("sgd0"), bf("sgd1")]
        B_kwo = [bf("kwo0"), bf("kwo1")]; B_kwn = [bf("kwn0"), bf("kwn1")]
        B_vwo = [bf("vwo0"), bf("vwo1")]; B_vwn = [bf("vwn0"), bf("vwn1")]
        B_VB = [bf("vb%d" % i) for i in range(16)]
        B_ptl = [bf("pt%d" % i) for i in range(6)]
        B_PS = [bf("ps%d" % i) for i in range(7)]
        B_PT = bf("pt")
        B_XD = [bf("xd%d" % t) for t in range(NG * 4)]

        def fs(ap):
            n = 1
            for d in ap.shape[1:]:
                n *= d
            return n

        def dma(out, in_, R, W, eng="sp", is_out=False):
            nbytes = fs(out) * out.shape[0] * 4
            P.add(eng, lambda e: e.dma_start(out=out, in_=in_), R, W, dma=True, is_out=is_out,
                  cost=(150.0 if eng == "sp" else 1500.0), lat=2500.0 + nbytes / 100.0)

        def act(out, in_, func, R, W, **kw):
            P.add("act", lambda e: e.activation(out=out, in_=in_, func=func, **kw), R, W,
                  cost=220.0 + fs(out) / 1.4, lat=250.0)

        def vcost(eng, out):
            if eng == "dve":
                return 120.0 + fs(out) / 0.96
            return 350.0 + 1.9 * fs(out)

        def tt(eng, out, in0, in1, op, R, W):
            P.add(eng, lambda e: e.tensor_tensor(out=out, in0=in0, in1=in1, op=op), R, W, cost=vcost(eng, out), lat=250.0)

        def ts(eng, out, in0, s1, s2, op0, op1, R, W):
            if s2 is None:
                P.add(eng, lambda e: e.tensor_scalar(out=out, in0=in0, scalar1=s1, scalar2=None, op0=op0), R, W,
                      cost=vcost(eng, out), lat=250.0)
            else:
                P.add(eng, lambda e: e.tensor_scalar(out=out, in0=in0, scalar1=s1, scalar2=s2, op0=op0, op1=op1),
                      R, W, cost=vcost(eng, out), lat=250.0)

        def stt(eng, out, in0, scalar, in1, op0, op1, R, W):
            P.add(eng, lambda e: e.scalar_tensor_tensor(out=out, in0=in0, scalar=scalar, in1=in1, op0=op0, op1=op1),
                  R, W, cost=vcost(eng, out), lat=250.0)

        def mm(out, lhsT, rhs, start, stop, R, W, skip=False):
            P.add("pe", lambda e: e.matmul(out, lhsT=lhsT, rhs=rhs, start=start, stop=stop,
                                           skip_group_check=skip), R, W, cost=60.0 + max(64, fs(rhs)) / 2.4, lat=250.0)

        def tr(out, in_, idn, R, W):
            P.add("pe", lambda e: e.transpose(out, in_, idn), R, W, cost=110.0, lat=250.0)

        def cp(eng, out, in_, R, W):
            P.add(eng, lambda e: e.tensor_copy(out=out, in_=in_), R, W, cost=vcost(eng, out), lat=250.0)

        def ms(eng, out, val, W):
            P.add(eng, lambda e: e.memset(out, val), (), W, cost=vcost(eng, out))

        def recip(out, in_, R, W):
            P.add("dve", lambda e: e.reciprocal(out=out, in_=in_), R, W, cost=150.0 + 6.3 * fs(out), lat=250.0)

        bank_rr = [0]

        def nb():
            i = bank_rr[0] % 7
            bank_rr[0] += 1
            return PS[i], B_PS[i]

        dma(cstf[:], cst_d[:, :, :], [], [B_cstf])
        cp("dve", ident[:], cstf[:, 0, :], [B_cstf], [B_ident])
        dma(tab[:], tab_d[:, :, :], [], [B_tab], eng="pool")
        dma(pp[:], pp_d[:, :, :], [], [B_pp])
        dma(fgb[:], fgb_d[:, :], [], [B_fgb])
        ms("pool", vaug[:], 1.0, B_VB)
        for hh in range(2):
            ms("pool", Qbd[hh][:], 0.0, [B_Qbd[hh]])
        ms("pool", cpow[:, 0:1], -0.5, [B_cpow])
        ms("pool", cpow[:, 1:2], 0.5, [B_cpow])

        MUL, ADD = ALU.mult, ALU.add

        def ppow(out, in_, which, R, W, n=1):
            ex = cpow[:, which:which + 1] if n == 1 else cpow[:, which:which + 1].broadcast_to([128, n])
            P.add("pool", lambda e: e.tensor_tensor(out=out, in0=in_, in1=ex, op=ALU.pow), R + [B_cpow], W,
                  cost=250.0 + n / 0.96)

        def rmsnorm_stats(xap, bx, c):
            act(junk[:], xap, AF.Square, [bx], [B_junk, B_ss[c]], accum_out=ss[:, c:c + 1])
            ts("dve", rt[:, c:c + 1], ss[:, c:c + 1], 1.0 / D, EPS, MUL, ADD, [B_ss[c]], [B_rt[c]])
            ppow(rs[:, c:c + 1], rt[:, c:c + 1], 0, [B_rt[c]], [B_rs[c]])

        for l in range(L):
            last_layer = (l == L - 1)
            for c in range(13):
                for kc in range(8):
                    dma(w_in[:, kc, c * 256:(c + 1) * 256],
                        win_d[l, kc * 128:(kc + 1) * 128, c * 256:(c + 1) * 256], [], [B_win[c]], eng="pool")
            for kc in range(8):
                dma(w_out[:, kc, :], wout_d[l, kc * 128:(kc + 1) * 128, :], [], [B_wout], eng="pool")
            dma(ngb[:], ngb_d[l, :, :], [], [B_ngb])
            dma(wbd[:], wbd_d[l, :, :, :], [], [B_wbd], eng="pool")
            dma(ggb[:], ggb_d[l, :, :], [], [B_ggb])
            dma(wstf[:], wst_d[l, :, :, :], [], [B_wstf], eng="pool")
            dma(bsb[:], bsb_d[l, :, :, 0:128], [], [B_bsb])
            for h in range(4):
                tt("dve", wst[:, h, :], wstf[:, h, :], cstf[:, 1, :], MUL, [B_wstf, B_cstf], [B_wst])
            act(lct[:], pp[:, l, 20:22], AF.Exp, [B_pp], [B_lct], scale=-1.0)
            act(lct[:], lct[:], AF.Ln, [B_lct], [B_lct], bias=1.0)
            ts("dve", lc[:], lct[:], -8.0, None, MUL, None, [B_lct], [B_lc])
            ts("dve", lch[:], lct[:], -4.0, None, MUL, None, [B_lct], [B_lch])
            ts("dve", bh[:], pp[:, l, 16:20], 0.5, None, MUL, None, [B_pp], [B_bh])
            ts("pool", ggb[:], ggb[:], 0.5, None, MUL, None, [B_ggb], [B_ggb])
            for hh in range(2):
                ms("pool", tA[hh][:, 0:2], 0.0, [B_tA[hh]])
                ms("pool", xr[hh][:, 0:3], 0.0, [B_xr[hh]])
                ms("pool", kwin[hh][:, 0:2048], 0.0, [B_kwo[hh]])
                ms("pool", vwin[hh][:, 0:2048], 0.0, [B_vwo[hh]])
            ms("pool", hst[:], 0.0, [B_hst])

            def PPc(col):
                return pp[:, l, col:col + 1]

            for g in range(NG):
                src = x_d if l == 0 else xs_d
                hT = hTs[0]
                YT = YTs[g % 2]
                B_YT = B_YTs[g % 2]
                B_hT = B_hTs[0]
                for j in range(4):
                    T = g * 4 + j
                    xb_ = xa[T % 2]
                    bx = B_xa[T % 2]
                    c = T % 8
                    dma(xb_[:], src[T * 128:(T + 1) * 128, :], [B_XD[T]] if l > 0 else [], [bx])
                    rmsnorm_stats(xb_[:], bx, c)
                    hb = ht[j % 2]
                    stt("dve", hb[:], xb_[:], rs[:, c:c + 1], ngb[:], MUL, MUL,
                        [bx, B_rs[c], B_ngb], [B_ht[j % 2]])
                    for kc in range(8):
                        tr(PT[:, kc * 128:(kc + 1) * 128], hb[:, kc * 128:(kc + 1) * 128], ident[:],
                           [B_ht[j % 2], B_ident], [B_PT])
                    P.add("act", (lambda jj, hTt: (lambda e: e.activation(
                        out=hTt[:, :, jj * 128:(jj + 1) * 128],
                        in_=PT[:, :].rearrange("p (k t) -> p k t", k=8), func=AF.Copy)))(j, hT),
                        [B_PT], [B_hT[j]], cost=220.0 + 1024 / 1.4)

                def S_(hh, t):
                    return SC[hh * 5 + t], B_SC[hh * 5 + t]

                def inproj(ch, hh):
                    bk, bb = nb()
                    ct = ch * 2 + hh
                    for kc in range(8):
                        mm(bk[:], w_in[:, kc, ct * 128:(ct + 1) * 128], hT[:, kc, :], kc == 0, kc == 7,
                           [B_win[ch]] + B_hT, [bb])
                    return bk, bb

                def silu2(dst, bdst, pg):
                    act(dst, pg[0][:], AF.Tanh, [pg[1]], [bdst], scale=0.5)
                    stt("dve", dst, dst, 1.0, pg[0][:], ADD, MUL, [bdst, pg[1]], [bdst])

                def gelu2(src, n, t_in, b_in, t_out, b_out):
                    act(t_in, src[0], AF.Square, [src[1]], [b_in])
                    ts("dve", t_in, t_in, 0.044715, 1.0, MUL, ADD, [b_in], [b_in])
                    tt("dve", t_in, t_in, src[0], MUL, [b_in, src[1]], [b_in])
                    act(t_out, t_in, AF.Tanh, [b_in], [b_out], scale=0.7978845608028654)
                    stt("dve", t_out, t_out, 1.0, src[0], ADD, MUL, [b_out, src[1]], [b_out])

                for hh in range(2):
                    px = inproj(0, hh)
                    pb = inproj(1, hh)
                    pc = inproj(2, hh)
                    pg = inproj(3, hh)
                    (s0, b0), (s1, b1) = S_(hh, 0), S_(hh, 1)
                    act(s0[:], px[0][:], AF.Copy, [px[1]], [b0])
                    tt("dve", tA[hh][:, 2:2 + G], pc[0][:], s0[:], MUL, [pc[1], b0], [B_tA[hh]])
                    silu2(s1[:], b1, pg)
                    ts("dve", s0[:], tA[hh][:, 2:2 + G], PPc(hh * 3 + 2), None, MUL, None, [B_tA[hh], B_pp], [b0])
                    stt("dve", s0[:], tA[hh][:, 1:1 + G], PPc(hh * 3 + 1), s0[:], MUL, ADD, [B_tA[hh], B_pp, b0], [b0])
                    stt("dve", s0[:], tA[hh][:, 0:G], PPc(hh * 3 + 0), s0[:], MUL, ADD, [B_tA[hh], B_pp, b0], [b0])
                    cp("pool", tA[hh][:, 0:2], tA[hh][:, G:G + 2], [B_tA[hh]], [B_tA[hh]])
                    tt("dve", s0[:], s0[:], pb[0][:], MUL, [b0, pb[1]], [b0])
                    stt("dve", YT[0 + hh][:], s0[:], 0.5, s1[:], MUL, MUL, [b0, b1], [B_YT[0 + hh]])

                for hh in range(2):
                    px = inproj(4, hh)
                    pg = inproj(5, hh)
                    (sc_, bc_), (sa_, ba_), (sm_, bm_), (si_, bi_), (sh_, bh_) = [S_(hh, t) for t in range(5)]
                    xbb = xbbs[hh]
                    B_xbb = B_xbbs[hh]
                    act(xr[hh][:, 3:3 + G], px[0][:], AF.Copy, [px[1]], [B_xr[hh]])
                    cb = 6 + hh * 4
                    ts("dve", sc_[:], xr[hh][:, 3:3 + G], PPc(cb + 3), PPc(14 + hh), MUL, ADD, [B_xr[hh], B_pp], [bc_])
                    for k in (2, 1, 0):
                        stt("dve", sc_[:], xr[hh][:, k:k + G], PPc(cb + k), sc_[:], MUL, ADD,
                            [B_xr[hh], B_pp, bc_], [bc_])
                    cp("pool", xr[hh][:, 0:3], xr[hh][:, G:G + 3], [B_xr[hh]], [B_xr[hh]])
                    act(xbb[:], sc_[:], AF.Copy, [bc_], [B_xbb])
                    bkr, bbr = nb()
                    mm(bkr[:], wbd[:, 0 + hh, :], xbb[:], True, True, [B_wbd, B_xbb], [bbr])
                    bki, bbi = nb()
                    mm(bki[:], wbd[:, 2 + hh, :], xbb[:], True, True, [B_wbd, B_xbb], [bbi])
                    act(sa_[:], bkr[:], AF.Tanh, [bbr, B_bh], [ba_], scale=0.5, bias=bh[:, hh:hh + 1])
                    act(si_[:], bki[:], AF.Tanh, [bbi, B_bh], [bi_], scale=0.5, bias=bh[:, 2 + hh:3 + hh])
                    act(sm_[:], sa_[:], AF.Exp, [ba_, B_lc], [bm_], scale=lc[:, hh:hh + 1], bias=lc[:, hh:hh + 1])
                    act(sa_[:], sa_[:], AF.Exp, [ba_, B_lch], [ba_], scale=lch[:, hh:hh + 1], bias=lch[:, hh:hh + 1])
                    ts("dve", sm_[:], sm_[:], 1.0, 1.0, ALU.min, ALU.subtract, [bm_], [bm_])
                    act(sm_[:], sm_[:], AF.Sqrt, [bm_], [bm_], scale=-1.0)
                    stt("dve", si_[:], si_[:], 1.0, sc_[:], ADD, MUL, [bi_, bc_], [bi_])
                    stt("dve", sm_[:], sm_[:], 0.5, si_[:], MUL, MUL, [bm_, bi_], [bm_])
                    P.add("dve", (lambda hh_, a_, m_, h_: (lambda e: e.tensor_tensor_scan(
                        out=h_[:], data0=a_[:], data1=m_[:], initial=hst[:, hh_:hh_ + 1],
                        op0=MUL, op1=ADD)))(hh, sa_, sm_, sh_),
                        [ba_, bm_, B_hst], [bh_], cost=120.0 + 2 * G / 0.96)
                    cp("pool", hst[:, hh:hh + 1], sh_[:, G - 1:G], [bh_], [B_hst])
                    silu2(si_[:], bi_, pg)
                    stt("dve", YT[2 + hh][:], sh_[:], 0.5, si_[:], MUL, MUL, [bh_, bi_], [B_YT[2 + hh]])

                for j in range(4):
                    bk, bb = nb()
                    for kc in range(8):
                        mm(bk[:, 0:256], hT[:, kc, j * 128:(j + 1) * 128], w_in[:, kc, 7 * 256:8 * 256],
                           kc == 0, kc == 7, [B_win[7]] + B_hT, [bb])
                    (s3, b3), (s4, b4) = S_(j % 2, 3), S_(j % 2, 4)
                    gelu2((bk[:, 0:256], bb), 256, s3[:, 0:256], b3, s4[:, 0:256], b4)
                    act(junk[:, 0:256], s4[:, 0:256], AF.Square, [b4], [B_junk, B_ssv], accum_out=ssv[:, j:j + 1])
                    ts("dve", rtv[:, j:j + 1], ssv[:, j:j + 1], 0.25 / 256, EPS, MUL, ADD, [B_ssv], [B_rtv])
                    ppow(rsv[:, j:j + 1], rtv[:, j:j + 1], 0, [B_rtv], [B_rsv])
                    stt("dve", vvt[j][:], s4[:, 0:256], rsv[:, j:j + 1], ggb[:], MUL, MUL,
                        [b4, B_rsv, B_ggb], [B_vvt[j]])
                for hh in range(2):
                    pu = inproj(6, hh)
                    pg = inproj(8, hh)
                    bks, bbs = nb()
                    for j in range(4):
                        for hl in range(2):
                            h = hh * 2 + hl
                            mm(bks[hl * 64:(hl + 1) * 64, j * 128:(j + 1) * 128],
                               vvt[j][:, h * 64:(h + 1) * 64], wst[:, h, :], True, True,
                               [B_vvt[j], B_wst], [bbs], skip=True)
                    (s0, b0), (s1, b1), (s2, b2) = S_(hh, 0), S_(hh, 1), S_(hh, 2)
                    silu2(s2[:], b2, pg)
                    gelu2((pu[0][:], pu[1]), G, s0[:], b0, s1[:], b1)
                    tt("dve", s0[:, :].rearrange("p (a b) -> p a b", b=128), bks[:, :].rearrange("p (a b) -> p a b", b=128),
                       bsb[:, hh, :].unsqueeze(1).broadcast_to([128, 4, 128]), ADD, [bbs, B_bsb], [b0])
                    tt("dve", s0[:], s0[:], s2[:], MUL, [b0, b2], [b0])
                    stt("dve", YT[4 + hh][:], s0[:], 0.25, s1[:], MUL, MUL, [b0, b1], [B_YT[4 + hh]])

                for hh in range(2):
                    if g > 0:
                        for c in range(4):
                            act(kwin[hh][:, c * 512:(c + 1) * 512], kwin[hh][:, (c + 1) * 512:(c + 2) * 512], AF.Copy,
                                [B_kwo[hh] if c < 3 else B_kwn[hh]], [B_kwo[hh]])
                            act(vwin[hh][:, c * 512:(c + 1) * 512], vwin[hh][:, (c + 1) * 512:(c + 2) * 512], AF.Copy,
                                [B_vwo[hh] if c < 3 else B_vwn[hh]], [B_vwo[hh]])
                    pz = {}
                    for nm, ch in (("q", 9), ("k", 10), ("v", 11), ("g", 12)):
                        bk, bb = nb()
                        ct = ch * 2 + hh
                        for kc in range(8):
                            mm(bk[:], w_in[:, kc, ct * 128:(ct + 1) * 128], hT[:, kc, :], kc == 0, kc == 7,
                               [B_win[ch]] + B_hT, [bb])
                        pz[nm] = (bk, bb)
                    act(Qbd[hh][0:64, 0, :], pz["q"][0][0:64, :], AF.Copy, [pz["q"][1]], [B_Qbd[hh]])
                    cp("dve", Qbd[hh][64:128, 1, :], pz["q"][0][64:128, :], [pz["q"][1]], [B_Qbd[hh]])
                    act(kwin[hh][:, 2048:2560], pz["k"][0][:], AF.Copy, [pz["k"][1]], [B_kwn[hh]])
                    act(vwin[hh][:, 2048:2560], pz["v"][0][:], AF.Copy, [pz["v"][1]], [B_vwn[hh]])
                    silu2(sgd[hh][:], B_sgd[hh], pz["g"])

                for hh in range(2):
                    acc = [(PS[2 * hh], B_PS[2 * hh]), (PS[2 * hh + 1], B_PS[2 * hh + 1])]
                    first = [True, True]
                    srr = [0]

                    def sbank():
                        i = 4 + (srr[0] % 3)
                        srr[0] += 1
                        return PS[i], B_PS[i]

                    prr = [0]
                    for pi, dil in ((2, 16), (1, 4), (0, 1)):
                        blocks = []
                        if dil == 1:
                            for b in range(-1, 4):
                                if b == -1 and g == 0:
                                    continue
                                blocks.append((b + 1, slice(2048 + 128 * b, 2048 + 128 * b + 128, 1), 128))
                        elif dil == 4:
                            for r in range(4):
                                if g > 0:
                                    blocks.append((r, slice(1536 + r, 2048, 4), 128))
                            for r in range(4):
                                blocks.append((4 + r, slice(2048 + r, 2560, 4), 128))
                        else:
                            for r in range(16):
                                if g > 0:
                                    blocks.append((r, slice(r, 2048, 16), 128))
                        i0 = 0
                        while i0 < len(blocks):
                            batch = [blocks[i0]]
                            i1 = i0 + 1
                            while (i1 < len(blocks) and len(batch) < 8 and blocks[i1][2] == batch[0][2]
                                   and blocks[i1][0] == batch[-1][0] + 1
                                   and (blocks[i1][1].start < 2048) == (batch[0][1].start < 2048)):
                                batch.append(blocks[i1])
                                i1 += 1
                            nk = batch[0][2]
                            for bi, (blk, sl, _) in enumerate(batch):
                                tr(PT[0:nk, bi * 128:(bi + 1) * 128], vwin[hh][:, sl], ident[:],
                                   [B_vwo[hh] if sl.start < 2048 else B_vwn[hh], B_ident], [B_PT])
                            b0 = batch[0][0]
                            nbk = len(batch)
                            ptv = PT[0:nk, 0:nbk * 128].rearrange("p (b c) -> p b c", b=nbk)
                            cp("act" if False else "dve", vaug[0:nk, b0:b0 + nbk, 0, 0:64], ptv[:, :, 0:64],
                               [B_PT], [B_VB[b] for b in range(b0, b0 + nbk)])
                            P.add("act", (lambda o, i: (lambda e: e.activation(out=o, in_=i, func=AF.Copy)))(
                                vaug[0:nk, b0:b0 + nbk, 1, 64:128], ptv[:, :, 64:128]),
                                [B_PT], [B_VB[b] for b in range(b0, b0 + nbk)], cost=220.0 + nbk * 64 / 1.4)
                            i0 = i1

                        nsub = 4 if dil < 16 else 16
                        wq = G // nsub
                        nloc = nsub // 2
                        hasA = g > 0 or dil == 1
                        nkB = 32 if dil == 16 else 128
                        K = kwin[hh]

                        def subinfo(s_):
                            if dil == 1:
                                return (slice(s_ * 128, (s_ + 1) * 128, 1),
                                        slice(2048 + 128 * (s_ - 1), 2048 + 128 * s_, 1),
                                        slice(2048 + 128 * s_, 2048 + 128 * (s_ + 1), 1),
                                        not (g == 0 and s_ == 0), s_, s_ + 1)
                            if dil == 4:
                                return (slice(s_, 512, 4), slice(1536 + s_, 2048, 4), slice(2048 + s_, 2560, 4),
                                        g > 0, s_, 4 + s_)
                            return (slice(s_, 512, 16), slice(s_, 2048, 16), slice(2048 + s_, 2560, 16),
                                    g > 0, s_, 16 + s_)

                        hasB = dil < 16
                        for bi in range(2):
                            subs = range(bi * nloc, (bi + 1) * nloc)
                            ptiles = {}
                            for ab in (0, 1):
                                if (ab == 0 and not hasA) or (ab == 1 and not hasB):
                                    continue
                                if ab == 0 and not any(subinfo(s_)[3] for s_ in subs):
                                    continue
                                bank, bbank = sbank()
                                npart = 128 if ab == 0 else nkB
                                for s_ in subs:
                                    qs, kA, kB, okA, blkA, blkB = subinfo(s_)
                                    sl = s_ - bi * nloc
                                    cs = slice(sl * 2 * wq, (sl + 1) * 2 * wq)
                                    rhs = Qbd[hh][:, :, qs]
                                    if ab == 0:
                                        if okA:
                                            mm(bank[:, cs], K[:, kA], rhs, True, True,
                                               [B_kwo[hh] if kA.start < 2048 else B_kwn[hh], B_Qbd[hh]], [bbank], skip=True)
                                    else:
                                        mm(bank[0:nkB, cs], K[:, kB], rhs, True, True,
                                           [B_kwn[hh], B_Qbd[hh]], [bbank], skip=True)
                                ti = (hh * 2) * 6 + pi * 2 + ab
                                k_ = prr[0] % NPT
                                prr[0] += 1
                                pt_, bpt_ = ptl[k_], B_ptl[k_]
                                c0 = 2 * wq if (ab == 0 and g == 0 and dil == 1 and bi == 0) else 0
                                act(pt_[0:npart, c0:G], bank[0:npart, c0:G], AF.Exp, [bbank], [bpt_], scale=0.125)
                                nl = (G - c0) // (2 * wq)
                                tt("dve", pt_[0:npart, c0:G].rearrange("p (a h b) -> p a h b", h=2, b=wq),
                                   pt_[0:npart, c0:G].rearrange("p (a h b) -> p a h b", h=2, b=wq),
                                   tab[0:npart, ti:ti + 7:6, 0:wq].unsqueeze(1).broadcast_to([npart, nl, 2, wq]), MUL,
                                   [bpt_, B_tab], [bpt_])
                                if ab == 0 and dil == 16 and 0 < g < 4:
                                    ms("pool", pt_[0:32 * (4 - g), :], 0.0, [bpt_])
                                ptiles[ab] = (pt_, bpt_)
                            for hl in range(2):
                                accb, accB = acc[hl]
                                for s_ in subs:
                                    qs, kA, kB, okA, blkA, blkB = subinfo(s_)
                                    sl = s_ - bi * nloc
                                    cs = slice(sl * 2 * wq + hl * wq, sl * 2 * wq + (hl + 1) * wq)
                                    if hasA and okA:
                                        pa_, bpa_ = ptiles[0]
                                        mm(accb[:, qs], vaug[:, blkA, hl, :], pa_[:, cs], first[hl], True,
                                           [B_VB[blkA], bpa_], [accB], skip=True)
                                        first[hl] = False
                                    if hasB:
                                        pb_, bpb_ = ptiles[1]
                                        mm(accb[:, qs], vaug[0:nkB, blkB, hl, :], pb_[0:nkB, cs], first[hl], True,
                                           [B_VB[blkB], bpb_], [accB], skip=True)
                                        first[hl] = False
                    sf, bfz = S_(hh, 4)
                    act(sf[0:64, :], acc[0][0][64:128, :], AF.Copy, [acc[0][1]], [bfz])
                    act(sf[64:128, :], acc[1][0][0:64, :], AF.Copy, [acc[1][1]], [bfz])
                    recip(sf[:], sf[:], [bfz], [bfz])
                    tt("dve", sf[:], sf[:], sgd[hh][:], MUL, [bfz, B_sgd[hh]], [bfz])
                    stt("dve", YT[6 + hh][0:64, :], acc[0][0][0:64, :], 0.5, sf[0:64, :], MUL, MUL,
                        [acc[0][1], bfz], [B_YT[6 + hh]])
                    stt("dve", YT[6 + hh][64:128, :], acc[1][0][64:128, :], 0.5, sf[64:128, :], MUL, MUL,
                        [acc[1][1], bfz], [B_YT[6 + hh]])

                for j in range(4):
                    T = g * 4 + j
                    xo = xb2[T % 2]
                    bxo = B_xb[T % 2]
                    dma(xo[:], src[T * 128:(T + 1) * 128, :], [B_XD[T]] if l > 0 else [], [bxo])
                    for nh in range(2):
                        bk, bb = nb()
                        for ct in range(8):
                            mm(bk[:], YT[ct][:, j * 128:(j + 1) * 128], w_out[:, ct, nh * 512:(nh + 1) * 512],
                               ct == 0, ct == 7, [B_YT[ct], B_wout], [bb])
                        tt("dve", xo[:, nh * 512:(nh + 1) * 512], bk[:], xo[:, nh * 512:(nh + 1) * 512], ADD,
                           [bb, bxo], [bxo])
                    if not last_layer:
                        dma(xs_d[T * 128:(T + 1) * 128, :], xo[:], [bxo], [B_XD[T]])
                    else:
                        c = 8 + T % 8
                        rmsnorm_stats(xo[:], bxo, c)
                        stt("dve", xo[:], xo[:], rs[:, c:c + 1], fgb[:], MUL, MUL,
                            [bxo, B_rs[c], B_fgb], [bxo])
                        dma(out_d[T * 128:(T + 1) * 128, :], xo[:], [bxo], [], is_out=True)
        P.emit(reorder=REORDER)
        build_nc.makespan = getattr(P, 'makespan', None)
    return nc


def _const_tables():
    slopes = [2.0 ** (-2.0 * (h + 1)) for h in range(4)]
    k = np.arange(128)[:, None]
    q = np.arange(128)[None, :]
    NEG = -240000.0
    tab = np.zeros((128, 24, 128), np.float32)
    for h in range(4):
        for pi, dil in enumerate((1, 4, 16)):
            dA = q + 128 - k
            tab[:, h * 6 + pi * 2 + 0, :] = np.where(k >= q, -8.0 * slopes[h] * dil * dA, NEG)
            dB = q - k
            tab[:, h * 6 + pi * 2 + 1, :] = np.where(q >= k, -8.0 * slopes[h] * dil * dB, NEG)
    cst = np.zeros((128, 2, 128), np.float32)
    cst[:, 0, :] = np.eye(128, dtype=np.float32)
    cst[:, 1, :] = (q >= k).astype(np.float32)
    tab = np.exp(tab / 8.0).astype(np.float32)
    both = ((q >= k) & ((q - k) % 4 == 0)).astype(np.float32)
    for h in range(4):
        tab[:, h * 6 + 3, :] *= (1.0 + both)
    return tab, cst


def _prep_shared(norm_g, w_in, conv_a_w, conv_r_w, conv_r_b, lru_wa, lru_ba, lru_wx, lru_bx,
                 lru_lambda, gmlp_norm_g, gmlp_ws, gmlp_bs, w_out, final_g):
    L = w_in.shape[0]
    f = np.float32
    pp = np.zeros((128, L, NPP), f)
    for hh in range(2):
        sl = slice(hh * 128, (hh + 1) * 128)
        for k in range(3):
            pp[:, :, hh * 3 + k] = conv_a_w[:, k, sl].T
        for k in range(4):
            pp[:, :, 6 + hh * 4 + k] = conv_r_w[:, k, sl].T
        pp[:, :, 14 + hh] = conv_r_b[:, sl].T
        pp[:, :, 16 + hh] = lru_ba[:, sl].T
        pp[:, :, 18 + hh] = lru_bx[:, sl].T
        pp[:, :, 20 + hh] = lru_lambda[:, sl].T
    wbd = np.zeros((L, 128, 4, 128), f)
    for gi, w in enumerate((lru_wa, lru_wx)):
        for h in range(4):
            hh, hl = h // 2, h % 2
            wbd[:, hl * 64:(hl + 1) * 64, gi * 2 + hh, hl * 64:(hl + 1) * 64] = w[:, h]
    wst = np.ascontiguousarray(np.transpose(gmlp_ws, (0, 3, 1, 2)))
    bsb = np.zeros((L, 128, 2, 512), f)
    for hh in range(2):
        for hl in range(2):
            bsb[:, hl * 64:(hl + 1) * 64, hh, :] = np.tile(gmlp_bs[:, hh * 2 + hl, :], (1, 4))[:, None, :]
    tab, cst = _const_tables()
    return {
        "w_in": np.ascontiguousarray(w_in, dtype=f),
        "w_out": np.ascontiguousarray(w_out, dtype=f),
        "ngb": np.ascontiguousarray(np.broadcast_to(norm_g[:, None, :], (L, 128, D)), dtype=f),
        "fgb": np.ascontiguousarray(np.broadcast_to(final_g[None, :], (128, D)), dtype=f),
        "pp": pp, "wbd": wbd,
        "ggb": np.ascontiguousarray(np.broadcast_to(gmlp_norm_g[:, None, :], (L, 128, 256)), dtype=f),
        "wst": wst, "bsb": bsb, "tab": tab, "cst": cst,
    }


_NC_CACHE = {}


def run(x, params, n_cores):
    x = np.asarray(x, np.float32)
    Bn, S, _ = x.shape
    L = params["w_in"].shape[0]
    shared = _prep_shared(**{k: np.asarray(v, np.float32) for k, v in params.items()})
    key = (L, S)
    if key not in _NC_CACHE:
        _NC_CACHE[key] = build_nc(L, S)
    nc = _NC_CACHE[key]
    in_maps = []
    for c in range(n_cores):
        m = dict(shared)
        m["x"] = np.ascontiguousarray(x[c % Bn])
        in_maps.append(m)
    res = run_bass_kernel_spmd(nc, in_maps, core_ids=list(range(n_cores)))
    return np.stack([res.results[b]["out"] for b in range(Bn)], axis=0).astype(np.float32)


def kernel(x, norm_g, w_in, conv_a_w, conv_r_w, conv_r_b, lru_wa, lru_ba, lru_wx, lru_bx,
           lru_lambda, gmlp_norm_g, gmlp_ws, gmlp_bs, w_out, final_g):
    params = dict(norm_g=norm_g, w_in=w_in, conv_a_w=conv_a_w, conv_r_w=conv_r_w, conv_r_b=conv_r_b,
                  lru_wa=lru_wa, lru_ba=lru_ba, lru_wx=lru_wx, lru_bx=lru_bx, lru_lambda=lru_lambda,
                  gmlp_norm_g=gmlp_norm_g, gmlp_ws=gmlp_ws, gmlp_bs=gmlp_bs, w_out=w_out, final_g=final_g)
    return run(x, params, 4)
```

```python
import numpy as np
from contextlib import ExitStack
import concourse.bass as bass
import concourse.mybir as mybir
from concourse.bass_utils import run_bass_kernel_spmd

F32 = mybir.dt.float32
BF16 = mybir.dt.bfloat16
AF = mybir.ActivationFunctionType
ALU = mybir.AluOpType

D = 1024
DIN = 3328
G = 512
NPP = 22
EPS = 1e-6
REORDER = True


class Buf:
    __slots__ = ("name", "w", "r")

    def __init__(self, name):
        self.name = name
        self.w = None
        self.r = []


class Op:
    __slots__ = ("eng", "fn", "deps", "is_dma", "needs_inc", "inc_idx",
                 "dma_sem", "dma_val", "clk", "cost", "lat", "idx", "t0", "t1", "succ", "npend", "prevdma")

    def __init__(self, eng, fn, is_dma):
        self.eng = eng
        self.fn = fn
        self.is_dma = is_dma
        self.deps = []
        self.needs_inc = False
        self.inc_idx = 0
        self.dma_sem = -1
        self.dma_val = 0
        self.clk = None
        self.succ = []
        self.prevdma = None


class Prog:
    ENGS = ("pe", "act", "dve", "pool", "sp")
    NDMA = {"sp": 16, "pool": 8, "act": 4}

    def __init__(self, nc):
        self.nc = nc
        self.all = []
        self.out_dmas = []
        self.last_dma = {}

    def add(self, eng, fn, reads=(), writes=(), dma=False, is_out=False, cost=200.0, lat=0.0):
        op = Op(eng, fn, dma)
        op.cost = cost
        op.lat = lat
        op.idx = len(self.all)
        deps = {}
        for b in reads:
            if b.w is not None:
                deps[id(b.w)] = (b.w, True)
        for b in writes:
            if b.w is not None and id(b.w) not in deps:
                deps[id(b.w)] = (b.w, False)
            for r in b.r:
                if id(r) not in deps:
                    deps[id(r)] = (r, False)
        for d, raw in deps.values():
            if d is op:
                continue
            op.deps.append((d, raw))
        for b in reads:
            b.r.append(op)
        for b in writes:
            b.w = op
            b.r = []
        self.all.append(op)
        if is_out:
            self.out_dmas.append(op)
        return op

    def schedule(self):
        import heapq
        ops = self.all
        for op in ops:
            op.npend = len(op.deps) + (1 if op.prevdma is not None else 0)
            for d, _ in op.deps:
                d.succ.append(op)
        nxt = {}
        for op in ops:
            if op.prevdma is not None:
                nxt[id(op.prevdma)] = op
        bl = {}
        for op in reversed(ops):
            m = 0.0
            for s2 in op.succ:
                v = bl[id(s2)]
                if v > m:
                    m = v
            bl[id(op)] = op.cost + op.lat + m
        for op in ops:
            op.idx = (-bl[id(op)], op.idx)
        ready = {e: [] for e in self.ENGS}
        rtime = {}
        for op in ops:
            if op.npend == 0:
                heapq.heappush(ready[op.eng], (op.idx, op))
                rtime[id(op)] = 0.0
        free = {e: 0.0 for e in self.ENGS}
        order = []
        nleft = len(ops)
        while nleft:
            best = None
            for e in self.ENGS:
                h = ready[e]
                if not h:
                    continue
                cand = heapq.nsmallest(6, h)
                pick = None
                for idx, op in cand:
                    if rtime[id(op)] <= free[e]:
                        pick = op
                        break
                if pick is None:
                    pick = min((c[1] for c in cand), key=lambda o: (rtime[id(o)], o.idx))
                st = max(free[e], rtime[id(pick)])
                if best is None or st < best[0] or (st == best[0] and pick.idx < best[2].idx):
                    best = (st, e, pick)
            st, e, op = best
            h = ready[e]
            h.remove((op.idx, op))
            heapq.heapify(h)
            op.t0 = st
            free[e] = st + op.cost
            op.t1 = st + op.cost + op.lat
            order.append(op)
            nleft -= 1
            rel = list(op.succ)
            for s2 in rel:
                s2.npend -= 1
                rtime[id(s2)] = max(rtime.get(id(s2), 0.0), op.t1)
                if s2.npend == 0:
                    heapq.heappush(ready[s2.eng], (s2.idx, s2))
            n2 = nxt.get(id(op))
            if n2 is not None:
                n2.npend -= 1
                rtime[id(n2)] = max(rtime.get(id(n2), 0.0), op.t0)
                if n2.npend == 0:
                    heapq.heappush(ready[n2.eng], (n2.idx, n2))
        self.makespan = max(o.t1 for o in order)
        return order

    def emit(self, reorder=True):
        nc = self.nc
        if reorder:
            order = self.schedule()
        else:
            order = list(self.all)
        self.ops = {e: [] for e in self.ENGS}
        for op in order:
            self.ops[op.eng].append(op)
        dma_rr = {e: 0 for e in self.ENGS}
        dma_cnt = {}
        dma_prev = {}
        sem_deps = {}
        for op in order:
            sd = []
            for d, raw in op.deps:
                if d.is_dma or d.eng != op.eng or op.eng != "pe":
                    sd.append(d)
                    d.needs_inc = True
            if op.is_dma:
                n = self.NDMA[op.eng]
                s = dma_rr[op.eng] % n
                dma_rr[op.eng] += 1
                key = (op.eng, s)
                prev = dma_prev.get(key)
                if prev is not None:
                    sd.append(prev)
                dma_cnt[key] = dma_cnt.get(key, 0) + 1
                op.dma_sem = key
                op.dma_val = 16 * dma_cnt[key]
                dma_prev[key] = op
            sem_deps[id(op)] = sd
        cnt = {e: 0 for e in self.ENGS}
        for e in self.ENGS:
            for op in self.ops[e]:
                if op.is_dma:
                    continue
                if op.needs_inc:
                    cnt[e] += 1
                op.inc_idx = cnt[e]
        known = {e: {} for e in self.ENGS}
        waits = {}
        for op in order:
            kn = known[op.eng]
            m = {}
            for d in sem_deps[id(op)]:
                if d.is_dma:
                    key = ("dma",) + d.dma_sem
                    val = d.dma_val
                else:
                    key = d.eng
                    val = d.inc_idx
                if kn.get(key, 0) >= val:
                    continue
                m[key] = max(m.get(key, 0), val)
                for k, v in d.clk.items():
                    if kn.get(k, 0) < v:
                        kn[k] = v
                if kn.get(key, 0) < val:
                    kn[key] = val
            waits[id(op)] = list(m.items())
            c = dict(kn)
            if not op.is_dma:
                c[op.eng] = max(c.get(op.eng, 0), op.inc_idx)
            op.clk = c

        with ExitStack() as st:
            sems = {e: st.enter_context(nc.semaphore("s_" + e)) for e in self.ENGS}
            dsems = {}
            for e, n in self.NDMA.items():
                for i in range(n):
                    dsems[(e, i)] = st.enter_context(nc.semaphore("d_%s%d" % (e, i)))
            block = st.enter_context(nc.Block())

            def body(ename):
                def f(eng):
                    for op in self.ops[ename]:
                        for key, val in waits[id(op)]:
                            if isinstance(key, tuple):
                                eng.wait_ge(dsems[key[1:]], val)
                            else:
                                eng.wait_ge(sems[key], val)
                        ins = op.fn(eng)
                        if op.is_dma:
                            ins.then_inc(dsems[op.dma_sem], 16)
                        elif op.needs_inc:
                            ins.then_inc(sems[ename], 1)
                    if ename == "sp":
                        last = {}
                        for op in self.out_dmas:
                            last[op.dma_sem] = max(last.get(op.dma_sem, 0), op.dma_val)
                        for s, v in last.items():
                            eng.wait_ge(dsems[s], v)
                return f

            block.tensor(body("pe"))
            block.scalar(body("act"))
            block.vector(body("dve"))
            block.gpsimd(body("pool"))
            block.sync(body("sp"))


def build_nc(L, S):
    NG = S // G
    nc = bass.Bass("TRN2", target_bir_lowering=False)
    dt = lambda n, s, d=F32, k="ExternalInput": nc.dram_tensor(n, s, d, kind=k).ap()
    x_d = dt("x", [S, D])
    win_d = dt("w_in", [L, D, DIN])
    wout_d = dt("w_out", [L, D, D])
    ngb_d = dt("ngb", [L, 128, D])
    fgb_d = dt("fgb", [128, D])
    pp_d = dt("pp", [128, L, NPP])
    wbd_d = dt("wbd", [L, 128, 4, 128])
    ggb_d = dt("ggb", [L, 128, 256])
    wst_d = dt("wst", [L, 128, 4, 128])
    bsb_d = dt("bsb", [L, 128, 2, 512])
    tab_d = dt("tab", [128, 24, 128])
    cst_d = dt("cst", [128, 2, 128])
    out_d = dt("out", [S, D], F32, "ExternalOutput")
    xs_d = dt("xs", [S, D], F32, "Internal")

    with ExitStack() as st:
        def sb(n, s, d=F32):
            return st.enter_context(nc.sbuf_tensor(n, s, d))

        def ps(n, s, d=F32):
            return st.enter_context(nc.psum_tensor(n, s, d))

        P = Prog(nc)

        w_in = sb("w_in_sb", [128, 8, DIN], BF16)
        w_out = sb("w_out_sb", [128, 8, D], BF16)
        xa = [sb("xa%d" % i, [128, D]) for i in range(2)]
        xb2 = [sb("xb%d" % i, [128, D]) for i in range(2)]
        ht = [sb("ht%d" % i, [128, D], BF16) for i in range(2)]
        hTs = [sb("hT%d" % i, [128, 8, G], BF16) for i in range(1)]
        ngb = sb("ngb_sb", [128, D])
        fgb = sb("fgb_sb", [128, D])
        pp = sb("pp_sb", [128, L, NPP])
        lc = sb("lc", [128, 2])
        lct = sb("lct", [128, 2])
        wbd = sb("wbd_sb", [128, 4, 128], BF16)
        ggb = sb("ggb_sb", [128, 256])
        wstf = sb("wstf", [128, 4, 128], BF16)
        wst = sb("wst_sb", [128, 4, 128], BF16)
        bsb = sb("bsb_sb", [128, 2, 128])
        tab = sb("tab_sb", [128, 24, 128], BF16)
        cstf = sb("cstf", [128, 2, 128])
        ident = sb("ident", [128, 128], BF16)
        ss = sb("ss", [128, 16])
        rt = sb("rt", [128, 16])
        rs = sb("rs", [128, 16])
        ssv = sb("ssv", [128, 4])
        rtv = sb("rtv", [128, 4])
        rsv = sb("rsv", [128, 4])
        NSC = 10
        SC = [sb("sc%d" % i, [128, G]) for i in range(NSC)]
        tA = [sb("tA%d" % i, [128, 2 + G]) for i in range(2)]
        xr = [sb("xr%d" % i, [128, 3 + G]) for i in range(2)]
        hst = sb("hst", [128, 2])
        xbbs = [sb("xbb%d" % i, [128, G], BF16) for i in range(2)]
        cpow = sb("cpow", [128, 2])
        lch = sb("lch", [128, 2])
        bh = sb("bh", [128, 4])
        vvt = [sb("vvt%d" % i, [128, 256], BF16) for i in range(4)]
        YTs = [[sb("yT%d_%d" % (p, i), [128, G], BF16) for i in range(8)] for p in range(2)]
        Qbd = [sb("Qbd%d" % i, [128, 2, G], BF16) for i in range(2)]
        sgd = [sb("sgd%d" % i, [128, G]) for i in range(2)]
        kwin = [sb("kwin%d" % i, [128, 2560], BF16) for i in range(2)]
        vwin = [sb("vwin%d" % i, [128, 2560], BF16) for i in range(2)]
        vaug = sb("vaug", [128, 16, 2, 128], BF16)
        NPT = 6
        ptl = [sb("pt%d" % i, [128, G], BF16) for i in range(NPT)]
        PS = [ps("ps%d" % i, [128, G]) for i in range(7)]
        PT = ps("pt", [128, 1024], BF16)

        bf = Buf
        B_win = [bf("w_in%d" % c) for c in range(13)]
        B_wout = bf("w_out")
        B_xa = [bf("xa0"), bf("xa1")]
        B_xb = [bf("xb0"), bf("xb1")]
        B_ht = [bf("ht0"), bf("ht1")]
        B_hTs = [[bf("hT%d_%d" % (p, i)) for i in range(4)] for p in range(1)]
        B_ngb = bf("ngb"); B_fgb = bf("fgb"); B_pp = bf("pp"); B_lc = bf("lc"); B_lct = bf("lct")
        B_wbd = bf("wbd"); B_ggb = bf("ggb"); B_wstf = bf("wstf"); B_wst = bf("wst"); B_bsb = bf("bsb")
        B_tab = bf("tab"); B_cstf = bf("cstf"); B_ident = bf("ident")
        junk = ht[1]; B_junk = B_ht[1]
        B_ss = [bf("ss%d" % i) for i in range(16)]; B_rt = [bf("rt%d" % i) for i in range(16)]; B_rs = [bf("rs%d" % i) for i in range(16)]
        B_ssv = bf("ssv"); B_rtv = bf("rtv"); B_rsv = bf("rsv")
        B_SC = [bf("sc%d" % i) for i in range(NSC)]
        B_tA = [bf("tA0"), bf("tA1")]
        B_xr = [bf("xr0"), bf("xr1")]
        B_hst = bf("hst"); B_xbbs = [bf("xbb0"), bf("xbb1")]; B_cpow = bf("cpow"); B_lch = bf("lch"); B_bh = bf("bh")
        B_vvt = [bf("vvt%d" % i) for i in range(4)]
        B_YTs = [[bf("yT%d_%d" % (p, i)) for i in range(8)] for p in range(2)]
        B_Qbd = [bf("Qbd0"), bf("Qbd1")]
        B_sgd = [bf("sgd0"), bf("sgd1")]
        B_kwo = [bf("kwo0"), bf("kwo1")]; B_kwn = [bf("kwn0"), bf("kwn1")]
        B_vwo = [bf("vwo0"), bf("vwo1")]; B_vwn = [bf("vwn0"), bf("vwn1")]
        B_VB = [bf("vb%d" % i) for i in range(16)]
        B_ptl = [bf("pt%d" % i) for i in range(6)]
        B_PS = [bf("ps%d" % i) for i in range(7)]
        B_PT = bf("pt")
        B_XD = [bf("xd%d" % t) for t in range(NG * 4)]

        def fs(ap):
            n = 1
            for d in ap.shape[1:]:
                n *= d
            return n

        def dma(out, in_, R, W, eng="sp", is_out=False):
            nbytes = fs(out) * out.shape[0] * 4
            P.add(eng, lambda e: e.dma_start(out=out, in_=in_), R, W, dma=True, is_out=is_out,
                  cost=(150.0 if eng == "sp" else 1500.0), lat=2500.0 + nbytes / 100.0)

        def act(out, in_, func, R, W, **kw):
            P.add("act", lambda e: e.activation(out=out, in_=in_, func=func, **kw), R, W,
                  cost=220.0 + fs(out) / 1.4, lat=250.0)

        def vcost(eng, out):
            if eng == "dve":
                return 120.0 + fs(out) / 0.96
            return 350.0 + 1.9 * fs(out)

        def tt(eng, out, in0, in1, op, R, W):
            P.add(eng, lambda e: e.tensor_tensor(out=out, in0=in0, in1=in1, op=op), R, W, cost=vcost(eng, out), lat=250.0)

        def ts(eng, out, in0, s1, s2, op0, op1, R, W):
            if s2 is None:
                P.add(eng, lambda e: e.tensor_scalar(out=out, in0=in0, scalar1=s1, scalar2=None, op0=op0), R, W,
                      cost=vcost(eng, out), lat=250.0)
            else:
                P.add(eng, lambda e: e.tensor_scalar(out=out, in0=in0, scalar1=s1, scalar2=s2, op0=op0, op1=op1),
                      R, W, cost=vcost(eng, out), lat=250.0)

        def stt(eng, out, in0, scalar, in1, op0, op1, R, W):
            P.add(eng, lambda e: e.scalar_tensor_tensor(out=out, in0=in0, scalar=scalar, in1=in1, op0=op0, op1=op1),
                  R, W, cost=vcost(eng, out), lat=250.0)

        def mm(out, lhsT, rhs, start, stop, R, W, skip=False):
            P.add("pe", lambda e: e.matmul(out, lhsT=lhsT, rhs=rhs, start=start, stop=stop,
                                           skip_group_check=skip), R, W, cost=60.0 + max(64, fs(rhs)) / 2.4, lat=250.0)

        def tr(out, in_, idn, R, W):
            P.add("pe", lambda e: e.transpose(out, in_, idn), R, W, cost=110.0, lat=250.0)

        def cp(eng, out, in_, R, W):
            P.add(eng, lambda e: e.tensor_copy(out=out, in_=in_), R, W, cost=vcost(eng, out), lat=250.0)

        def ms(eng, out, val, W):
            P.add(eng, lambda e: e.memset(out, val), (), W, cost=vcost(eng, out))

        def recip(out, in_, R, W):
            P.add("dve", lambda e: e.reciprocal(out=out, in_=in_), R, W, cost=150.0 + 6.3 * fs(out), lat=250.0)

        bank_rr = [0]

        def nb():
            i = bank_rr[0] % 7
            bank_rr[0] += 1
            return PS[i], B_PS[i]

        dma(cstf[:], cst_d[:, :, :], [], [B_cstf])
        cp("dve", ident[:], cstf[:, 0, :], [B_cstf], [B_ident])
        dma(tab[:], tab_d[:, :, :], [], [B_tab], eng="pool")
        dma(pp[:], pp_d[:, :, :], [], [B_pp])
        dma(fgb[:], fgb_d[:, :], [], [B_fgb])
        ms("pool", vaug[:], 1.0, B_VB)
        for hh in range(2):
            ms("pool", Qbd[hh][:], 0.0, [B_Qbd[hh]])
        ms("pool", cpow[:, 0:1], -0.5, [B_cpow])
        ms("pool", cpow[:, 1:2], 0.5, [B_cpow])

        MUL, ADD = ALU.mult, ALU.add

        def ppow(out, in_, which, R, W, n=1):
            ex = cpow[:, which:which + 1] if n == 1 else cpow[:, which:which + 1].broadcast_to([128, n])
            P.add("pool", lambda e: e.tensor_tensor(out=out, in0=in_, in1=ex, op=ALU.pow), R + [B_cpow], W,
                  cost=250.0 + n / 0.96)

        def rmsnorm_stats(xap, bx, c):
            act(junk[:], xap, AF.Square, [bx], [B_junk, B_ss[c]], accum_out=ss[:, c:c + 1])
            ts("dve", rt[:, c:c + 1], ss[:, c:c + 1], 1.0 / D, EPS, MUL, ADD, [B_ss[c]], [B_rt[c]])
            ppow(rs[:, c:c + 1], rt[:, c:c + 1], 0, [B_rt[c]], [B_rs[c]])

        for l in range(L):
            last_layer = (l == L - 1)
            for c in range(13):
                for kc in range(8):
                    dma(w_in[:, kc, c * 256:(c + 1) * 256],
                        win_d[l, kc * 128:(kc + 1) * 128, c * 256:(c + 1) * 256], [], [B_win[c]], eng="pool")
            for kc in range(8):
                dma(w_out[:, kc, :], wout_d[l, kc * 128:(kc + 1) * 128, :], [], [B_wout], eng="pool")
            dma(ngb[:], ngb_d[l, :, :], [], [B_ngb])
            dma(wbd[:], wbd_d[l, :, :, :], [], [B_wbd], eng="pool")
            dma(ggb[:], ggb_d[l, :, :], [], [B_ggb])
            dma(wstf[:], wst_d[l, :, :, :], [], [B_wstf], eng="pool")
            dma(bsb[:], bsb_d[l, :, :, 0:128], [], [B_bsb])
            for h in range(4):
                tt("dve", wst[:, h, :], wstf[:, h, :], cstf[:, 1, :], MUL, [B_wstf, B_cstf], [B_wst])
            act(lct[:], pp[:, l, 20:22], AF.Exp, [B_pp], [B_lct], scale=-1.0)
            act(lct[:], lct[:], AF.Ln, [B_lct], [B_lct], bias=1.0)
            ts("dve", lc[:], lct[:], -8.0, None, MUL, None, [B_lct], [B_lc])
            ts("dve", lch[:], lct[:], -4.0, None, MUL, None, [B_lct], [B_lch])
            ts("dve", bh[:], pp[:, l, 16:20], 0.5, None, MUL, None, [B_pp], [B_bh])
            ts("pool", ggb[:], ggb[:], 0.5, None, MUL, None, [B_ggb], [B_ggb])
            for hh in range(2):
                ms("pool", tA[hh][:, 0:2], 0.0, [B_tA[hh]])
                ms("pool", xr[hh][:, 0:3], 0.0, [B_xr[hh]])
                ms("pool", kwin[hh][:, 0:2048], 0.0, [B_kwo[hh]])
                ms("pool", vwin[hh][:, 0:2048], 0.0, [B_vwo[hh]])
            ms("pool", hst[:], 0.0, [B_hst])

            def PPc(col):
                return pp[:, l, col:col + 1]

            for g in range(NG):
                src = x_d if l == 0 else xs_d
                hT = hTs[0]
                YT = YTs[g % 2]
                B_YT = B_YTs[g % 2]
                B_hT = B_hTs[0]
                for j in range(4):
                    T = g * 4 + j
                    xb_ = xa[T % 2]
                    bx = B_xa[T % 2]
                    c = T % 8
                    dma(xb_[:], src[T * 128:(T + 1) * 128, :], [B_XD[T]] if l > 0 else [], [bx])
                    rmsnorm_stats(xb_[:], bx, c)
                    hb = ht[j % 2]
                    stt("dve", hb[:], xb_[:], rs[:, c:c + 1], ngb[:], MUL, MUL,
                        [bx, B_rs[c], B_ngb], [B_ht[j % 2]])
                    for kc in range(8):
                        tr(PT[:, kc * 128:(kc + 1) * 128], hb[:, kc * 128:(kc + 1) * 128], ident[:],
                           [B_ht[j % 2], B_ident], [B_PT])
                    P.add("act", (lambda jj, hTt: (lambda e: e.activation(
                        out=hTt[:, :, jj * 128:(jj + 1) * 128],
                        in_=PT[:, :].rearrange("p (k t) -> p k t", k=8), func=AF.Copy)))(j, hT),
                        [B_PT], [B_hT[j]], cost=220.0 + 1024 / 1.4)

                def S_(hh, t):
                    return SC[hh * 5 + t], B_SC[hh * 5 + t]

                def inproj(ch, hh):
                    bk, bb = nb()
                    ct = ch * 2 + hh
                    for kc in range(8):
                        mm(bk[:], w_in[:, kc, ct * 128:(ct + 1) * 128], hT[:, kc, :], kc == 0, kc == 7,
                           [B_win[ch]] + B_hT, [bb])
                    return bk, bb

                def silu2(dst, bdst, pg):
                    act(dst, pg[0][:], AF.Tanh, [pg[1]], [bdst], scale=0.5)
                    stt("dve", dst, dst, 1.0, pg[0][:], ADD, MUL, [bdst, pg[1]], [bdst])

                def gelu2(src, n, t_in, b_in, t_out, b_out):
                    act(t_in, src[0], AF.Square, [src[1]], [b_in])
                    ts("dve", t_in, t_in, 0.044715, 1.0, MUL, ADD, [b_in], [b_in])
                    tt("dve", t_in, t_in, src[0], MUL, [b_in, src[1]], [b_in])
                    act(t_out, t_in, AF.Tanh, [b_in], [b_out], scale=0.7978845608028654)
                    stt("dve", t_out, t_out, 1.0, src[0], ADD, MUL, [b_out, src[1]], [b_out])

                for hh in range(2):
                    px = inproj(0, hh)
                    pb = inproj(1, hh)
                    pc = inproj(2, hh)
                    pg = inproj(3, hh)
                    (s0, b0), (s1, b1) = S_(hh, 0), S_(hh, 1)
                    act(s0[:], px[0][:], AF.Copy, [px[1]], [b0])
                    tt("dve", tA[hh][:, 2:2 + G], pc[0][:], s0[:], MUL, [pc[1], b0], [B_tA[hh]])
                    silu2(s1[:], b1, pg)
                    ts("dve", s0[:], tA[hh][:, 2:2 + G], PPc(hh * 3 + 2), None, MUL, None, [B_tA[hh], B_pp], [b0])
                    stt("dve", s0[:], tA[hh][:, 1:1 + G], PPc(hh * 3 + 1), s0[:], MUL, ADD, [B_tA[hh], B_pp, b0], [b0])
                    stt("dve", s0[:], tA[hh][:, 0:G], PPc(hh * 3 + 0), s0[:], MUL, ADD, [B_tA[hh], B_pp, b0], [b0])
                    cp("pool", tA[hh][:, 0:2], tA[hh][:, G:G + 2], [B_tA[hh]], [B_tA[hh]])
                    tt("dve", s0[:], s0[:], pb[0][:], MUL, [b0, pb[1]], [b0])
                    stt("dve", YT[0 + hh][:], s0[:], 0.5, s1[:], MUL, MUL, [b0, b1], [B_YT[0 + hh]])

                for hh in range(2):
                    px = inproj(4, hh)
                    pg = inproj(5, hh)
                    (sc_, bc_), (sa_, ba_), (sm_, bm_), (si_, bi_), (sh_, bh_) = [S_(hh, t) for t in range(5)]
                    xbb = xbbs[hh]
                    B_xbb = B_xbbs[hh]
                    act(xr[hh][:, 3:3 + G], px[0][:], AF.Copy, [px[1]], [B_xr[hh]])
                    cb = 6 + hh * 4
                    ts("dve", sc_[:], xr[hh][:, 3:3 + G], PPc(cb + 3), PPc(14 + hh), MUL, ADD, [B_xr[hh], B_pp], [bc_])
                    for k in (2, 1, 0):
                        stt("dve", sc_[:], xr[hh][:, k:k + G], PPc(cb + k), sc_[:], MUL, ADD,
                            [B_xr[hh], B_pp, bc_], [bc_])
                    cp("pool", xr[hh][:, 0:3], xr[hh][:, G:G + 3], [B_xr[hh]], [B_xr[hh]])
                    act(xbb[:], sc_[:], AF.Copy, [bc_], [B_xbb])
                    bkr, bbr = nb()
                    mm(bkr[:], wbd[:, 0 + hh, :], xbb[:], True, True, [B_wbd, B_xbb], [bbr])
                    bki, bbi = nb()
                    mm(bki[:], wbd[:, 2 + hh, :], xbb[:], True, True, [B_wbd, B_xbb], [bbi])
                    act(sa_[:], bkr[:], AF.Tanh, [bbr, B_bh], [ba_], scale=0.5, bias=bh[:, hh:hh + 1])
                    act(si_[:], bki[:], AF.Tanh, [bbi, B_bh], [bi_], scale=0.5, bias=bh[:, 2 + hh:3 + hh])
                    act(sm_[:], sa_[:], AF.Exp, [ba_, B_lc], [bm_], scale=lc[:, hh:hh + 1], bias=lc[:, hh:hh + 1])
                    act(sa_[:], sa_[:], AF.Exp, [ba_, B_lch], [ba_], scale=lch[:, hh:hh + 1], bias=lch[:, hh:hh + 1])
                    ts("dve", sm_[:], sm_[:], 1.0, 1.0, ALU.min, ALU.subtract, [bm_], [bm_])
                    act(sm_[:], sm_[:], AF.Sqrt, [bm_], [bm_], scale=-1.0)
                    stt("dve", si_[:], si_[:], 1.0, sc_[:], ADD, MUL, [bi_, bc_], [bi_])
                    stt("dve", sm_[:], sm_[:], 0.5, si_[:], MUL, MUL, [bm_, bi_], [bm_])
                    P.add("dve", (lambda hh_, a_, m_, h_: (lambda e: e.tensor_tensor_scan(
                        out=h_[:], data0=a_[:], data1=m_[:], initial=hst[:, hh_:hh_ + 1],
                        op0=MUL, op1=ADD)))(hh, sa_, sm_, sh_),
                        [ba_, bm_, B_hst], [bh_], cost=120.0 + 2 * G / 0.96)
                    cp("pool", hst[:, hh:hh + 1], sh_[:, G - 1:G], [bh_], [B_hst])
                    silu2(si_[:], bi_, pg)
                    stt("dve", YT[2 + hh][:], sh_[:], 0.5, si_[:], MUL, MUL, [bh_, bi_], [B_YT[2 + hh]])

                for j in range(4):
                    bk, bb = nb()
                    for kc in range(8):
                        mm(bk[:, 0:256], hT[:, kc, j * 128:(j + 1) * 128], w_in[:, kc, 7 * 256:8 * 256],
                           kc == 0, kc == 7, [B_win[7]] + B_hT, [bb])
                    (s3, b3), (s4, b4) = S_(j % 2, 3), S_(j % 2, 4)
                    gelu2((bk[:, 0:256], bb), 256, s3[:, 0:256], b3, s4[:, 0:256], b4)
                    act(junk[:, 0:256], s4[:, 0:256], AF.Square, [b4], [B_junk, B_ssv], accum_out=ssv[:, j:j + 1])
                    ts("dve", rtv[:, j:j + 1], ssv[:, j:j + 1], 0.25 / 256, EPS, MUL, ADD, [B_ssv], [B_rtv])
                    ppow(rsv[:, j:j + 1], rtv[:, j:j + 1], 0, [B_rtv], [B_rsv])
                    stt("dve", vvt[j][:], s4[:, 0:256], rsv[:, j:j + 1], ggb[:], MUL, MUL,
                        [b4, B_rsv, B_ggb], [B_vvt[j]])
                for hh in range(2):
                    pu = inproj(6, hh)
                    pg = inproj(8, hh)
                    bks, bbs = nb()
                    for j in range(4):
                        for hl in range(2):
                            h = hh * 2 + hl
                            mm(bks[hl * 64:(hl + 1) * 64, j * 128:(j + 1) * 128],
                               vvt[j][:, h * 64:(h + 1) * 64], wst[:, h, :], True, True,
                               [B_vvt[j], B_wst], [bbs], skip=True)
                    (s0, b0), (s1, b1), (s2, b2) = S_(hh, 0), S_(hh, 1), S_(hh, 2)
                    silu2(s2[:], b2, pg)
                    gelu2((pu[0][:], pu[1]), G, s0[:], b0, s1[:], b1)
                    tt("dve", s0[:, :].rearrange("p (a b) -> p a b", b=128), bks[:, :].rearrange("p (a b) -> p a b", b=128),
                       bsb[:, hh, :].unsqueeze(1).broadcast_to([128, 4, 128]), ADD, [bbs, B_bsb], [b0])
                    tt("dve", s0[:], s0[:], s2[:], MUL, [b0, b2], [b0])
                    stt("dve", YT[4 + hh][:], s0[:], 0.25, s1[:], MUL, MUL, [b0, b1], [B_YT[4 + hh]])

                for hh in range(2):
                    if g > 0:
                        for c in range(4):
                            act(kwin[hh][:, c * 512:(c + 1) * 512], kwin[hh][:, (c + 1) * 512:(c + 2) * 512], AF.Copy,
                                [B_kwo[hh] if c < 3 else B_kwn[hh]], [B_kwo[hh]])
                            act(vwin[hh][:, c * 512:(c + 1) * 512], vwin[hh][:, (c + 1) * 512:(c + 2) * 512], AF.Copy,
                                [B_vwo[hh] if c < 3 else B_vwn[hh]], [B_vwo[hh]])
                    pz = {}
                    for nm, ch in (("q", 9), ("k", 10), ("v", 11), ("g", 12)):
                        bk, bb = nb()
                        ct = ch * 2 + hh
                        for kc in range(8):
                            mm(bk[:], w_in[:, kc, ct * 128:(ct + 1) * 128], hT[:, kc, :], kc == 0, kc == 7,
                               [B_win[ch]] + B_hT, [bb])
                        pz[nm] = (bk, bb)
                    act(Qbd[hh][0:64, 0, :], pz["q"][0][0:64, :], AF.Copy, [pz["q"][1]], [B_Qbd[hh]])
                    cp("dve", Qbd[hh][64:128, 1, :], pz["q"][0][64:128, :], [pz["q"][1]], [B_Qbd[hh]])
                    act(kwin[hh][:, 2048:2560], pz["k"][0][:], AF.Copy, [pz["k"][1]], [B_kwn[hh]])
                    act(vwin[hh][:, 2048:2560], pz["v"][0][:], AF.Copy, [pz["v"][1]], [B_vwn[hh]])
                    silu2(sgd[hh][:], B_sgd[hh], pz["g"])

                for hh in range(2):
                    acc = [(PS[2 * hh], B_PS[2 * hh]), (PS[2 * hh + 1], B_PS[2 * hh + 1])]
                    first = [True, True]
                    srr = [0]

                    def sbank():
                        i = 4 + (srr[0] % 3)
                        srr[0] += 1
                        return PS[i], B_PS[i]

                    prr = [0]
                    for pi, dil in ((2, 16), (1, 4), (0, 1)):
                        blocks = []
                        if dil == 1:
                            for b in range(-1, 4):
                                if b == -1 and g == 0:
                                    continue
                                blocks.append((b + 1, slice(2048 + 128 * b, 2048 + 128 * b + 128, 1), 128))
                        elif dil == 4:
                            for r in range(4):
                                if g > 0:
                                    blocks.append((r, slice(1536 + r, 2048, 4), 128))
                            for r in range(4):
                                blocks.append((4 + r, slice(2048 + r, 2560, 4), 128))
                        else:
                            for r in range(16):
                                if g > 0:
                                    blocks.append((r, slice(r, 2048, 16), 128))
                        i0 = 0
                        while i0 < len(blocks):
                            batch = [blocks[i0]]
                            i1 = i0 + 1
                            while (i1 < len(blocks) and len(batch) < 8 and blocks[i1][2] == batch[0][2]
                                   and blocks[i1][0] == batch[-1][0] + 1
                                   and (blocks[i1][1].start < 2048) == (batch[0][1].start < 2048)):
                                batch.append(blocks[i1])
                                i1 += 1
                            nk = batch[0][2]
                            for bi, (blk, sl, _) in enumerate(batch):
                                tr(PT[0:nk, bi * 128:(bi + 1) * 128], vwin[hh][:, sl], ident[:],
                                   [B_vwo[hh] if sl.start < 2048 else B_vwn[hh], B_ident], [B_PT])
                            b0 = batch[0][0]
                            nbk = len(batch)
                            ptv = PT[0:nk, 0:nbk * 128].rearrange("p (b c) -> p b c", b=nbk)
                            act(vaug[0:nk, b0:b0 + nbk, 0, 0:64], ptv[:, :, 0:64], AF.Copy,
                                [B_PT], [B_VB[b] for b in range(b0, b0 + nbk)])
                            P.add("act", (lambda o, i: (lambda e: e.activation(out=o, in_=i, func=AF.Copy)))(
                                vaug[0:nk, b0:b0 + nbk, 1, 64:128], ptv[:, :, 64:128]),
                                [B_PT], [B_VB[b] for b in range(b0, b0 + nbk)], cost=220.0 + nbk * 64 / 1.4)
                            i0 = i1

                        nsub = 4 if dil < 16 else 16
                        wq = G // nsub
                        nloc = nsub // 2
                        hasA = g > 0 or dil == 1
                        nkB = 32 if dil == 16 else 128
                        K = kwin[hh]

                        def subinfo(s_):
                            if dil == 1:
                                return (slice(s_ * 128, (s_ + 1) * 128, 1),
                                        slice(2048 + 128 * (s_ - 1), 2048 + 128 * s_, 1),
                                        slice(2048 + 128 * s_, 2048 + 128 * (s_ + 1), 1),
                                        not (g == 0 and s_ == 0), s_, s_ + 1)
                            if dil == 4:
                                return (slice(s_, 512, 4), slice(1536 + s_, 2048, 4), slice(2048 + s_, 2560, 4),
                                        g > 0, s_, 4 + s_)
                            return (slice(s_, 512, 16), slice(s_, 2048, 16), slice(2048 + s_, 2560, 16),
                                    g > 0, s_, 16 + s_)

                        hasB = dil < 16
                        for bi in range(2):
                            subs = range(bi * nloc, (bi + 1) * nloc)
                            ptiles = {}
                            for ab in (0, 1):
                                if (ab == 0 and not hasA) or (ab == 1 and not hasB):
                                    continue
                                if ab == 0 and not any(subinfo(s_)[3] for s_ in subs):
                                    continue
                                bank, bbank = sbank()
                                npart = 128 if ab == 0 else nkB
                                for s_ in subs:
                                    qs, kA, kB, okA, blkA, blkB = subinfo(s_)
                                    sl = s_ - bi * nloc
                                    cs = slice(sl * 2 * wq, (sl + 1) * 2 * wq)
                                    rhs = Qbd[hh][:, :, qs]
                                    if ab == 0:
                                        if okA:
                                            mm(bank[:, cs], K[:, kA], rhs, True, True,
                                               [B_kwo[hh] if kA.start < 2048 else B_kwn[hh], B_Qbd[hh]], [bbank], skip=True)
                                    else:
                                        mm(bank[0:nkB, cs], K[:, kB], rhs, True, True,
                                           [B_kwn[hh], B_Qbd[hh]], [bbank], skip=True)
                                ti = (hh * 2) * 6 + pi * 2 + ab
                                k_ = prr[0] % NPT
                                prr[0] += 1
                                pt_, bpt_ = ptl[k_], B_ptl[k_]
                                c0 = 2 * wq if (ab == 0 and g == 0 and dil == 1 and bi == 0) else 0
                                act(pt_[0:npart, c0:G], bank[0:npart, c0:G], AF.Exp, [bbank], [bpt_], scale=0.125)
                                nl = (G - c0) // (2 * wq)
                                tt("dve", pt_[0:npart, c0:G].rearrange("p (a h b) -> p a h b", h=2, b=wq),
                                   pt_[0:npart, c0:G].rearrange("p (a h b) -> p a h b", h=2, b=wq),
                                   tab[0:npart, ti:ti + 7:6, 0:wq].unsqueeze(1).broadcast_to([npart, nl, 2, wq]), MUL,
                                   [bpt_, B_tab], [bpt_])
                                if ab == 0 and dil == 16 and 0 < g < 4:
                                    ms("pool", pt_[0:32 * (4 - g), :], 0.0, [bpt_])
                                ptiles[ab] = (pt_, bpt_)
                            for hl in range(2):
                                accb, accB = acc[hl]
                                for s_ in subs:
                                    qs, kA, kB, okA, blkA, blkB = subinfo(s_)
                                    sl = s_ - bi * nloc
                                    cs = slice(sl * 2 * wq + hl * wq, sl * 2 * wq + (hl + 1) * wq)
                                    if hasA and okA:
                                        pa_, bpa_ = ptiles[0]
                                        mm(accb[:, qs], vaug[:, blkA, hl, :], pa_[:, cs], first[hl], True,
                                           [B_VB[blkA], bpa_], [accB], skip=True)
                                        first[hl] = False
                                    if hasB:
                                        pb_, bpb_ = ptiles[1]
                                        mm(accb[:, qs], vaug[0:nkB, blkB, hl, :], pb_[0:nkB, cs], first[hl], True,
                                           [B_VB[blkB], bpb_], [accB], skip=True)
                                        first[hl] = False
                    sf, bfz = S_(hh, 4)
                    act(sf[0:64, :], acc[0][0][64:128, :], AF.Copy, [acc[0][1]], [bfz])
                    act(sf[64:128, :], acc[1][0][0:64, :], AF.Copy, [acc[1][1]], [bfz])
                    recip(sf[:], sf[:], [bfz], [bfz])
                    tt("dve", sf[:], sf[:], sgd[hh][:], MUL, [bfz, B_sgd[hh]], [bfz])
                    stt("dve", YT[6 + hh][0:64, :], acc[0][0][0:64, :], 0.5, sf[0:64, :], MUL, MUL,
                        [acc[0][1], bfz], [B_YT[6 + hh]])
                    stt("dve", YT[6 + hh][64:128, :], acc[1][0][64:128, :], 0.5, sf[64:128, :], MUL, MUL,
                        [acc[1][1], bfz], [B_YT[6 + hh]])

                for j in range(4):
                    T = g * 4 + j
                    xo = xb2[T % 2]
                    bxo = B_xb[T % 2]
                    dma(xo[:], src[T * 128:(T + 1) * 128, :], [B_XD[T]] if l > 0 else [], [bxo])
                    for nh in range(2):
                        bk, bb = nb()
                        for ct in range(8):
                            mm(bk[:], YT[ct][:, j * 128:(j + 1) * 128], w_out[:, ct, nh * 512:(nh + 1) * 512],
                               ct == 0, ct == 7, [B_YT[ct], B_wout], [bb])
                        tt("dve", xo[:, nh * 512:(nh + 1) * 512], bk[:], xo[:, nh * 512:(nh + 1) * 512], ADD,
                           [bb, bxo], [bxo])
                    if not last_layer:
                        dma(xs_d[T * 128:(T + 1) * 128, :], xo[:], [bxo], [B_XD[T]])
                    else:
                        c = 8 + T % 8
                        rmsnorm_stats(xo[:], bxo, c)
                        stt("dve", xo[:], xo[:], rs[:, c:c + 1], fgb[:], MUL, MUL,
                            [bxo, B_rs[c], B_fgb], [bxo])
                        dma(out_d[T * 128:(T + 1) * 128, :], xo[:], [bxo], [], is_out=True)
        P.emit(reorder=REORDER)
        build_nc.makespan = getattr(P, 'makespan', None)
    return nc


def _const_tables():
    slopes = [2.0 ** (-2.0 * (h + 1)) for h in range(4)]
    k = np.arange(128)[:, None]
    q = np.arange(128)[None, :]
    NEG = -240000.0
    tab = np.zeros((128, 24, 128), np.float32)
    for h in range(4):
        for pi, dil in enumerate((1, 4, 16)):
            dA = q + 128 - k
            tab[:, h * 6 + pi * 2 + 0, :] = np.where(k >= q, -8.0 * slopes[h] * dil * dA, NEG)
            dB = q - k
            tab[:, h * 6 + pi * 2 + 1, :] = np.where(q >= k, -8.0 * slopes[h] * dil * dB, NEG)
    cst = np.zeros((128, 2, 128), np.float32)
    cst[:, 0, :] = np.eye(128, dtype=np.float32)
    cst[:, 1, :] = (q >= k).astype(np.float32)
    tab = np.exp(tab / 8.0).astype(np.float32)
    both = ((q >= k) & ((q - k) % 4 == 0)).astype(np.float32)
    for h in range(4):
        tab[:, h * 6 + 3, :] *= (1.0 + both)
    return tab, cst


def _prep_shared(norm_g, w_in, conv_a_w, conv_r_w, conv_r_b, lru_wa, lru_ba, lru_wx, lru_bx,
                 lru_lambda, gmlp_norm_g, gmlp_ws, gmlp_bs, w_out, final_g):
    L = w_in.shape[0]
    f = np.float32
    pp = np.zeros((128, L, NPP), f)
    for hh in range(2):
        sl = slice(hh * 128, (hh + 1) * 128)
        for k in range(3):
            pp[:, :, hh * 3 + k] = conv_a_w[:, k, sl].T
        for k in range(4):
            pp[:, :, 6 + hh * 4 + k] = conv_r_w[:, k, sl].T
        pp[:, :, 14 + hh] = conv_r_b[:, sl].T
        pp[:, :, 16 + hh] = lru_ba[:, sl].T
        pp[:, :, 18 + hh] = lru_bx[:, sl].T
        pp[:, :, 20 + hh] = lru_lambda[:, sl].T
    wbd = np.zeros((L, 128, 4, 128), f)
    for gi, w in enumerate((lru_wa, lru_wx)):
        for h in range(4):
            hh, hl = h // 2, h % 2
            wbd[:, hl * 64:(hl + 1) * 64, gi * 2 + hh, hl * 64:(hl + 1) * 64] = w[:, h]
    wst = np.ascontiguousarray(np.transpose(gmlp_ws, (0, 3, 1, 2)))
    bsb = np.zeros((L, 128, 2, 512), f)
    for hh in range(2):
        for hl in range(2):
            bsb[:, hl * 64:(hl + 1) * 64, hh, :] = np.tile(gmlp_bs[:, hh * 2 + hl, :], (1, 4))[:, None, :]
    tab, cst = _const_tables()
    return {
        "w_in": np.ascontiguousarray(w_in, dtype=f),
        "w_out": np.ascontiguousarray(w_out, dtype=f),
        "ngb": np.ascontiguousarray(np.broadcast_to(norm_g[:, None, :], (L, 128, D)), dtype=f),
        "fgb": np.ascontiguousarray(np.broadcast_to(final_g[None, :], (128, D)), dtype=f),
        "pp": pp, "wbd": wbd,
        "ggb": np.ascontiguousarray(np.broadcast_to(gmlp_norm_g[:, None, :], (L, 128, 256)), dtype=f),
        "wst": wst, "bsb": bsb, "tab": tab, "cst": cst,
    }


_NC_CACHE = {}


def run(x, params, n_cores):
    x = np.asarray(x, np.float32)
    Bn, S, _ = x.shape
    L = params["w_in"].shape[0]
    shared = _prep_shared(**{k: np.asarray(v, np.float32) for k, v in params.items()})
    key = (L, S)
    if key not in _NC_CACHE:
        _NC_CACHE[key] = build_nc(L, S)
    nc = _NC_CACHE[key]
    in_maps = []
    for c in range(n_cores):
        m = dict(shared)
        m["x"] = np.ascontiguousarray(x[c % Bn])
        in_maps.append(m)
    res = run_bass_kernel_spmd(nc, in_maps, core_ids=list(range(n_cores)))
    return np.stack([res.results[b]["out"] for b in range(Bn)], axis=0).astype(np.float32)


def kernel(x, norm_g, w_in, conv_a_w, conv_r_w, conv_r_b, lru_wa, lru_ba, lru_wx, lru_bx,
           lru_lambda, gmlp_norm_g, gmlp_ws, gmlp_bs, w_out, final_g):
    params = dict(norm_g=norm_g, w_in=w_in, conv_a_w=conv_a_w, conv_r_w=conv_r_w, conv_r_b=conv_r_b,
                  lru_wa=lru_wa, lru_ba=lru_ba, lru_wx=lru_wx, lru_bx=lru_bx, lru_lambda=lru_lambda,
                  gmlp_norm_g=gmlp_norm_g, gmlp_ws=gmlp_ws, gmlp_bs=gmlp_bs, w_out=w_out, final_g=final_g)
    return run(x, params, 4)
```
